# Optimizing a Trainium2 kernel written in Bass

```python
import math
import jax
import jax.numpy as jnp
from jax import lax
import numpy as np

D_MODEL = 2048
BATCH = 4
SEQ = 2048
DEPTH = 4

N_MIXERS = 4
N_HEADS = 16
HEAD_DIM = 128
REL_BUCKETS = 32
REL_MAX_DIST = 128
BAND_BLOCK = 128
MOBA_BLOCK = 256
MOBA_TOPK = 3
MOBA_Q_CHUNK = 16
SWA_WINDOW = 128
SWA_KV_HEADS = 2
NSA_KV_HEADS = 4
NSA_CMP_LEN = 32
NSA_CMP_STRIDE = 16
NSA_CMP_HIDDEN = 256
NSA_SEL_BLOCK = 64
NSA_TOPN = 16
NSA_WINDOW = 512
NSA_Q_CHUNK = 32
MLA_Q_LORA = 512
MLA_KV_LORA = 512
MLA_NOPE_DIM = 128
MLA_ROPE_DIM = 64
MLA_V_DIM = 128
MLA_Q_BLOCK = 128
ROPE_BASE = 10000.0
D_FF = -(-8 * D_MODEL // 768) * 256
ALPHA = (2 * DEPTH) ** 0.25
BETA = (8 * DEPTH) ** -0.25
LN_EPS = 1e-5
RMS_EPS = 1e-6

kernel_name = 'hybrid_moba_swa_nsa_mla_deepnorm'


def layer_norm(x, g, b):
    xf = x.astype(jnp.float32)
    mu = jnp.mean(xf, axis=-1, keepdims=True)
    var = jnp.mean(jnp.square(xf - mu), axis=-1, keepdims=True)
    return ((xf - mu) * lax.rsqrt(var + LN_EPS) * g.astype(jnp.float32) + b.astype(jnp.float32)).astype(x.dtype)


def rms_norm(x, g):
    xf = x.astype(jnp.float32)
    y = xf * lax.rsqrt(jnp.mean(jnp.square(xf), axis=-1, keepdims=True) + RMS_EPS)
    return (y * g.astype(jnp.float32)).astype(x.dtype)


def t5_bucket(dist):
    n = jnp.maximum(dist, 0)
    max_exact = REL_BUCKETS // 2
    nf = jnp.maximum(n, max_exact).astype(jnp.float32)
    large = max_exact + (jnp.log(nf / max_exact) / math.log(REL_MAX_DIST / max_exact)
                         * (REL_BUCKETS - max_exact)).astype(jnp.int32)
    return jnp.where(n < max_exact, n, jnp.minimum(large, REL_BUCKETS - 1))


def masked_softmax(logits, mask):
    logits = jnp.where(mask, logits.astype(jnp.float32), -jnp.inf)
    m = jnp.max(logits, axis=-1, keepdims=True)
    m = jnp.where(jnp.isfinite(m), m, 0.0)
    p = jnp.exp(logits - m)
    return p / jnp.maximum(jnp.sum(p, axis=-1, keepdims=True), 1e-30)


def rope_tables(s):
    inv = ROPE_BASE ** (-jnp.arange(0, MLA_ROPE_DIM, 2, dtype=jnp.float32) / MLA_ROPE_DIM)
    ang = jnp.arange(s, dtype=jnp.float32)[:, None] * inv[None, :]
    return jnp.cos(ang), jnp.sin(ang)


def apply_rope(t, cos, sin):
    t1, t2 = jnp.split(t, 2, axis=-1)
    cos = cos.astype(t.dtype)
    sin = sin.astype(t.dtype)
    return jnp.concatenate([t1 * cos - t2 * sin, t1 * sin + t2 * cos], axis=-1)


def band_blocks(t, n_prev):
    b, hk, s, dh = t.shape
    nb = s // BAND_BLOCK
    tb = t.reshape(b, hk, nb, BAND_BLOCK, dh)
    tp = jnp.pad(tb, ((0, 0), (0, 0), (n_prev, 0), (0, 0), (0, 0)))
    return jnp.concatenate([tp[:, :, i:i + nb] for i in range(n_prev + 1)], axis=3)


def banded_gqa(q, k, v, window, bias_hg, sinks=None):
    b, hk, g, s, dh = q.shape
    nb = s // BAND_BLOCK
    n_prev = -(-(window - 1) // BAND_BLOCK)
    span = (n_prev + 1) * BAND_BLOCK
    kw = band_blocks(k, n_prev)
    vw = band_blocks(v, n_prev)
    qb = q.reshape(b, hk, g, nb, BAND_BLOCK, dh)
    logits = jnp.einsum('bkgnid,bknjd->bkgnij', qb, kw).astype(jnp.float32)
    i = jnp.arange(BAND_BLOCK)[:, None]
    j = jnp.arange(span)[None, :]
    dist = i + n_prev * BAND_BLOCK - j
    logits = logits + bias_hg[:, :, t5_bucket(dist)][None, :, :, None].astype(jnp.float32)
    qpos = (jnp.arange(nb) * BAND_BLOCK)[:, None, None] + i[None]
    mask = (dist >= 0) & (dist < window) & (qpos - dist >= 0)
    if sinks is None:
        p = masked_softmax(logits, mask)
    else:
        sink_col = jnp.broadcast_to(sinks.astype(jnp.float32)[None, :, :, None, None, None],
                                    logits.shape[:-1] + (1,))
        mask_s = jnp.pad(mask, ((0, 0), (0, 0), (0, 1)), constant_values=True)
        p = masked_softmax(jnp.concatenate([logits, sink_col], axis=-1), mask_s)[..., :span]
    out = jnp.einsum('bkgnij,bknjd->bkgnid', p.astype(v.dtype), vw)
    return out.reshape(b, hk, g, s, dh)


def moba_attention(q, k, v, bias_h):
    b, h, sp, dh = q.shape
    nb = sp // MOBA_BLOCK
    kb = k.reshape(b, h, nb, MOBA_BLOCK, dh)
    vb = v.reshape(b, h, nb, MOBA_BLOCK, dh)
    k_mean = jnp.mean(kb.astype(jnp.float32), axis=3)
    gate = jnp.einsum('bhsd,bhnd->bhsn', q.astype(jnp.float32), k_mean)
    q_blk = jnp.arange(sp) // MOBA_BLOCK
    fully_past = jnp.arange(nb)[None, :] < q_blk[:, None]
    gate = jnp.where(fully_past, gate, -jnp.inf)
    tk = max(1, min(MOBA_TOPK, nb - 1))
    _, sel = lax.top_k(gate, tk)
    nc = sp // MOBA_Q_CHUNK
    q_c = q.reshape(b, h, nc, MOBA_Q_CHUNK, dh).transpose(2, 0, 1, 3, 4)
    sel_c = sel.reshape(b, h, nc, MOBA_Q_CHUNK, tk).transpose(2, 0, 1, 3, 4)
    bi = jnp.arange(b)[:, None, None, None]
    hi = jnp.arange(h)[None, :, None, None]
    offs = jnp.arange(MOBA_BLOCK)
    n_sel = tk * MOBA_BLOCK

    def one_chunk(args):
        qc, selc, c = args
        qpos = c * MOBA_Q_CHUNK + jnp.arange(MOBA_Q_CHUNK)
        own = (c * MOBA_Q_CHUNK) // MOBA_BLOCK
        k_own = lax.dynamic_index_in_dim(kb, own, axis=2, keepdims=False)
        v_own = lax.dynamic_index_in_dim(vb, own, axis=2, keepdims=False)
        k_sel = kb[bi, hi, selc].reshape(b, h, MOBA_Q_CHUNK, n_sel, dh)
        v_sel = vb[bi, hi, selc].reshape(b, h, MOBA_Q_CHUNK, n_sel, dh)
        kpos_sel = (selc[..., None] * MOBA_BLOCK + offs).reshape(b, h, MOBA_Q_CHUNK, n_sel)
        kpos_own = own * MOBA_BLOCK + offs
        slot_ok = jnp.arange(tk)[None, :] < (qpos // MOBA_BLOCK)[:, None]
        mask_sel = jnp.repeat(slot_ok, MOBA_BLOCK, axis=-1)
        mask_own = kpos_own[None, :] <= qpos[:, None]
        l_sel = (jnp.einsum('bhqd,bhqjd->bhqj', qc, k_sel).astype(jnp.float32)
                 + bias_h[hi, t5_bucket(qpos[:, None] - kpos_sel)].astype(jnp.float32))
        l_own = (jnp.einsum('bhqd,bhjd->bhqj', qc, k_own).astype(jnp.float32)
                 + bias_h[:, t5_bucket(qpos[:, None] - kpos_own[None, :])].astype(jnp.float32))
        p = masked_softmax(jnp.concatenate([l_sel, l_own], axis=-1),
                           jnp.concatenate([mask_sel, mask_own], axis=-1)).astype(v.dtype)
        return (jnp.einsum('bhqj,bhqjd->bhqd', p[..., :n_sel], v_sel)
                + jnp.einsum('bhqj,bhjd->bhqd', p[..., n_sel:], v_own))

    out = lax.map(one_chunk, (q_c, sel_c, jnp.arange(nc)))
    return out.transpose(1, 2, 0, 3, 4).reshape(b, h, sp, dh)


def moba_mixer(x, w_qkv, w_o, rel_bias):
    b, s, _ = x.shape
    q, k, v = jnp.split(x @ w_qkv, 3, axis=-1)
    heads = lambda t: t.reshape(b, s, N_HEADS, HEAD_DIM).transpose(0, 2, 1, 3)
    q, k, v = heads(q) * HEAD_DIM ** -0.5, heads(k), heads(v)
    sp = -(-s // MOBA_BLOCK) * MOBA_BLOCK
    pad = ((0, 0), (0, 0), (0, sp - s), (0, 0))
    o = moba_attention(jnp.pad(q, pad), jnp.pad(k, pad), jnp.pad(v, pad), rel_bias.T)[:, :, :s]
    return o.transpose(0, 2, 1, 3).reshape(b, s, N_HEADS * HEAD_DIM) @ w_o


def swa_mixer(x, w_qkv, sinks, w_o, rel_bias):
    b, s, _ = x.shape
    g = N_HEADS // SWA_KV_HEADS
    q, k, v = jnp.split(x @ w_qkv, [N_HEADS * HEAD_DIM, (N_HEADS + SWA_KV_HEADS) * HEAD_DIM], axis=-1)
    q = q.reshape(b, s, SWA_KV_HEADS, g, HEAD_DIM).transpose(0, 2, 3, 1, 4) * HEAD_DIM ** -0.5
    k = k.reshape(b, s, SWA_KV_HEADS, HEAD_DIM).transpose(0, 2, 1, 3)
    v = v.reshape(b, s, SWA_KV_HEADS, HEAD_DIM).transpose(0, 2, 1, 3)
    bias_hg = rel_bias.T.reshape(SWA_KV_HEADS, g, REL_BUCKETS)
    o = banded_gqa(q, k, v, SWA_WINDOW, bias_hg, sinks.reshape(SWA_KV_HEADS, g))
    return o.transpose(0, 3, 1, 2, 4).reshape(b, s, N_HEADS * HEAD_DIM) @ w_o


def nsa_selected_attention(q, k, v, sel, sel_ok, bias_hg):
    b, hk, g, s, dh = q.shape
    n = sel.shape[-1]
    kb = k.reshape(b, hk, s // NSA_SEL_BLOCK, NSA_SEL_BLOCK, dh)
    vb = v.reshape(b, hk, s // NSA_SEL_BLOCK, NSA_SEL_BLOCK, dh)
    nc = s // NSA_Q_CHUNK
    q_c = q.reshape(b, hk, g, nc, NSA_Q_CHUNK, dh).transpose(3, 0, 1, 2, 4, 5)
    sel_c = sel.reshape(b, hk, nc, NSA_Q_CHUNK, n).transpose(2, 0, 1, 3, 4)
    ok_c = sel_ok.reshape(b, hk, nc, NSA_Q_CHUNK, n).transpose(2, 0, 1, 3, 4)
    bi = jnp.arange(b)[:, None, None, None]
    ki = jnp.arange(hk)[None, :, None, None]
    gi = jnp.arange(g)[None, None, :, None, None]
    offs = jnp.arange(NSA_SEL_BLOCK)

    def one_chunk(args):
        qc, selc, okc, c = args
        qpos = c * NSA_Q_CHUNK + jnp.arange(NSA_Q_CHUNK)
        ks = kb[bi, ki, selc].reshape(b, hk, NSA_Q_CHUNK, n * NSA_SEL_BLOCK, dh)
        vs = vb[bi, ki, selc].reshape(b, hk, NSA_Q_CHUNK, n * NSA_SEL_BLOCK, dh)
        kpos = (selc[..., None] * NSA_SEL_BLOCK + offs).reshape(b, hk, NSA_Q_CHUNK, n * NSA_SEL_BLOCK)
        dist = qpos[:, None] - kpos
        mask = (dist >= 0) & jnp.repeat(okc, NSA_SEL_BLOCK, axis=-1)
        bias = bias_hg[ki[..., None], gi, t5_bucket(dist)[:, :, None]].astype(jnp.float32)
        logits = jnp.einsum('bkgqd,bkqjd->bkgqj', qc, ks).astype(jnp.float32) + bias
        p = masked_softmax(logits, mask[:, :, None])
        return jnp.einsum('bkgqj,bkqjd->bkgqd', p.astype(vs.dtype), vs)

    out = lax.map(one_chunk, (q_c, sel_c, ok_c, jnp.arange(nc)))
    return out.transpose(1, 2, 3, 0, 4, 5).reshape(b, hk, g, s, dh)


def nsa_mixer(x, w_in, cmp_pos, cmp_w1, cmp_w2, w_o, rel_bias):
    b, s, _ = x.shape
    hk, dh = NSA_KV_HEADS, HEAD_DIM
    g = N_HEADS // hk
    kv_w = hk * dh
    splits = [int(c) for c in np.cumsum([N_HEADS * dh] + [kv_w] * 6)]
    parts = jnp.split(x @ w_in, splits, axis=-1)
    q = parts[0].reshape(b, s, hk, g, dh).transpose(0, 2, 3, 1, 4) * dh ** -0.5
    heads = lambda t: t.reshape(b, s, hk, dh).transpose(0, 2, 1, 3)
    k_cmp, v_cmp, k_slc, v_slc, k_win, v_win = [heads(t) for t in parts[1:7]]
    gates = jax.nn.sigmoid(parts[7].astype(jnp.float32)).reshape(b, s, hk, g, 3).transpose(0, 2, 3, 1, 4)
    bias_hg = rel_bias.T.reshape(hk, g, REL_BUCKETS)
    pos = jnp.arange(s)

    n_cmp = (s - NSA_CMP_LEN) // NSA_CMP_STRIDE + 1
    idx = jnp.arange(n_cmp)[:, None] * NSA_CMP_STRIDE + jnp.arange(NSA_CMP_LEN)[None, :]

    def compress(t, pe, w1, w2):
        blocks = t[:, :, idx] + pe
        hid = jax.nn.gelu(blocks.reshape(b, hk, n_cmp, NSA_CMP_LEN * dh) @ w1)
        return hid @ w2

    kc = compress(k_cmp, cmp_pos[0], cmp_w1[0], cmp_w2[0])
    vc = compress(v_cmp, cmp_pos[1], cmp_w1[1], cmp_w2[1])
    cmp_end = jnp.arange(n_cmp) * NSA_CMP_STRIDE + NSA_CMP_LEN - 1
    dist = pos[:, None] - cmp_end[None, :]
    logits = (jnp.einsum('bkgsd,bkcd->bkgsc', q, kc).astype(jnp.float32)
              + bias_hg[:, :, t5_bucket(dist)][None].astype(jnp.float32))
    p_cmp = masked_softmax(logits, dist >= 0)
    o_cmp = jnp.einsum('bkgsc,bkcd->bkgsd', p_cmp.astype(vc.dtype), vc)

    n_sel = s // NSA_SEL_BLOCK
    ci = jnp.arange(n_cmp)[:, None] * NSA_CMP_STRIDE
    sj = jnp.arange(n_sel)[None, :] * NSA_SEL_BLOCK
    overlap = ((ci < sj + NSA_SEL_BLOCK) & (ci + NSA_CMP_LEN > sj)).astype(jnp.float32)
    importance = jnp.einsum('bkgsc,cj->bksj', p_cmp, overlap)
    q_blk = (pos // NSA_SEL_BLOCK)[:, None]
    jj = jnp.arange(n_sel)[None, :]
    forced = (jj == 0) | (jj == q_blk) | (jj == q_blk - 1)
    importance = jnp.where(jj > q_blk, -jnp.inf, jnp.where(forced, jnp.inf, importance))
    top_val, sel = lax.top_k(importance, min(NSA_TOPN, n_sel))
    o_slc = nsa_selected_attention(q, k_slc, v_slc, sel, top_val > -jnp.inf, bias_hg)

    o_win = banded_gqa(q, k_win, v_win, NSA_WINDOW, bias_hg)

    o = (gates[..., 0:1] * o_cmp + gates[..., 1:2] * o_slc + gates[..., 2:3] * o_win).astype(x.dtype)
    return o.transpose(0, 3, 1, 2, 4).reshape(b, s, N_HEADS * dh) @ w_o


def mla_mixer(x, w_down, q_norm, kv_norm, w_uq, w_ukv, w_o):
    b, s, _ = x.shape
    h = N_HEADS
    c_q, c_kv, k_rope = jnp.split(x @ w_down, [MLA_Q_LORA, MLA_Q_LORA + MLA_KV_LORA], axis=-1)
    q = (rms_norm(c_q, q_norm) @ w_uq).reshape(b, s, h, MLA_NOPE_DIM + MLA_ROPE_DIM).transpose(0, 2, 1, 3)
    kv = (rms_norm(c_kv, kv_norm) @ w_ukv).reshape(b, s, h, MLA_NOPE_DIM + MLA_V_DIM).transpose(0, 2, 1, 3)
    cos, sin = rope_tables(s)
    scale = (MLA_NOPE_DIM + MLA_ROPE_DIM) ** -0.5
    q_nope = q[..., :MLA_NOPE_DIM] * scale
    q_rope = apply_rope(q[..., MLA_NOPE_DIM:], cos, sin) * scale
    k_nope, v = kv[..., :MLA_NOPE_DIM], kv[..., MLA_NOPE_DIM:]
    k_rope = apply_rope(k_rope, cos, sin)
    nqb = s // MLA_Q_BLOCK
    qn_c = q_nope.reshape(b, h, nqb, MLA_Q_BLOCK, MLA_NOPE_DIM).transpose(2, 0, 1, 3, 4)
    qr_c = q_rope.reshape(b, h, nqb, MLA_Q_BLOCK, MLA_ROPE_DIM).transpose(2, 0, 1, 3, 4)
    kpos = jnp.arange(s)

    def one_block(args):
        qn, qr, c = args
        qpos = c * MLA_Q_BLOCK + jnp.arange(MLA_Q_BLOCK)
        logits = (jnp.einsum('bhqd,bhkd->bhqk', qn, k_nope)
                  + jnp.einsum('bhqd,bkd->bhqk', qr, k_rope))
        p = masked_softmax(logits, kpos[None, :] <= qpos[:, None])
        return jnp.einsum('bhqk,bhkd->bhqd', p.astype(v.dtype), v)

    o = lax.map(one_block, (qn_c, qr_c, jnp.arange(nqb)))
    return o.transpose(1, 0, 3, 2, 4).reshape(b, s, h * MLA_V_DIM) @ w_o


def swiglu(x, w_gate, w_up, w_down):
    return (jax.nn.silu(x @ w_gate) * (x @ w_up)) @ w_down


def setup_inputs(seed: int = 0) -> dict:
    key = jax.random.key(seed)
    ks = iter(jax.random.split(key, 32))
    n_of = [(DEPTH - kind + N_MIXERS - 1) // N_MIXERS for kind in range(N_MIXERS)]
    na, nb_, nc, nd = n_of

    def nrm(shape, fan_in, gain=1.0):
        return jax.random.normal(next(ks), shape, jnp.float32) * (gain * fan_in ** -0.5)

    def noise(shape, scale):
        return jax.random.normal(next(ks), shape, jnp.float32) * scale

    hd = N_HEADS * HEAD_DIM
    nsa_in = hd + 6 * NSA_KV_HEADS * HEAD_DIM + 3 * N_HEADS
    return {
        'x': jax.random.normal(next(ks), (BATCH, SEQ, D_MODEL), jnp.float32),
        'rel_bias': noise((REL_BUCKETS, N_HEADS), 0.5),
        'moba_w_qkv': nrm((na, D_MODEL, 3 * hd), D_MODEL),
        'moba_w_o': nrm((na, hd, D_MODEL), hd, BETA),
        'swa_w_qkv': nrm((nb_, D_MODEL, (N_HEADS + 2 * SWA_KV_HEADS) * HEAD_DIM), D_MODEL),
        'swa_sinks': noise((nb_, N_HEADS), 1.0),
        'swa_w_o': nrm((nb_, hd, D_MODEL), hd, BETA),
        'nsa_w_in': nrm((nc, D_MODEL, nsa_in), D_MODEL),
        'nsa_cmp_pos': noise((nc, 2, NSA_CMP_LEN, HEAD_DIM), 0.1),
        'nsa_cmp_w1': nrm((nc, 2, NSA_CMP_LEN * HEAD_DIM, NSA_CMP_HIDDEN), NSA_CMP_LEN * HEAD_DIM),
        'nsa_cmp_w2': nrm((nc, 2, NSA_CMP_HIDDEN, HEAD_DIM), NSA_CMP_HIDDEN),
        'nsa_w_o': nrm((nc, hd, D_MODEL), hd, BETA),
        'mla_w_down': nrm((nd, D_MODEL, MLA_Q_LORA + MLA_KV_LORA + MLA_ROPE_DIM), D_MODEL),
        'mla_q_norm': 1.0 + noise((nd, MLA_Q_LORA), 0.02),
        'mla_kv_norm': 1.0 + noise((nd, MLA_KV_LORA), 0.02),
        'mla_w_uq': nrm((nd, MLA_Q_LORA, N_HEADS * (MLA_NOPE_DIM + MLA_ROPE_DIM)), MLA_Q_LORA),
        'mla_w_ukv': nrm((nd, MLA_KV_LORA, N_HEADS * (MLA_NOPE_DIM + MLA_V_DIM)), MLA_KV_LORA),
        'mla_w_o': nrm((nd, N_HEADS * MLA_V_DIM, D_MODEL), N_HEADS * MLA_V_DIM, BETA),
        'ln1_g': 1.0 + noise((DEPTH, D_MODEL), 0.02),
        'ln1_b': noise((DEPTH, D_MODEL), 0.02),
        'ffn_w_gate': nrm((DEPTH, D_MODEL, D_FF), D_MODEL),
        'ffn_w_up': nrm((DEPTH, D_MODEL, D_FF), D_MODEL),
        'ffn_w_down': nrm((DEPTH, D_FF, D_MODEL), D_FF, BETA),
        'ln2_g': 1.0 + noise((DEPTH, D_MODEL), 0.02),
        'ln2_b': noise((DEPTH, D_MODEL), 0.02),
    }


def reference(x, rel_bias, moba_w_qkv, moba_w_o, swa_w_qkv, swa_sinks, swa_w_o,
              nsa_w_in, nsa_cmp_pos, nsa_cmp_w1, nsa_cmp_w2, nsa_w_o,
              mla_w_down, mla_q_norm, mla_kv_norm, mla_w_uq, mla_w_ukv, mla_w_o,
              ln1_g, ln1_b, ffn_w_gate, ffn_w_up, ffn_w_down, ln2_g, ln2_b):
    for i in range(DEPTH):
        kind, j = i % N_MIXERS, i // N_MIXERS
        if kind == 0:
            h = moba_mixer(x, moba_w_qkv[j], moba_w_o[j], rel_bias)
        elif kind == 1:
            h = swa_mixer(x, swa_w_qkv[j], swa_sinks[j], swa_w_o[j], rel_bias)
        elif kind == 2:
            h = nsa_mixer(x, nsa_w_in[j], nsa_cmp_pos[j], nsa_cmp_w1[j], nsa_cmp_w2[j], nsa_w_o[j], rel_bias)
        else:
            h = mla_mixer(x, mla_w_down[j], mla_q_norm[j], mla_kv_norm[j], mla_w_uq[j], mla_w_ukv[j], mla_w_o[j])
        x = layer_norm(ALPHA * x + h, ln1_g[i], ln1_b[i])
        x = layer_norm(ALPHA * x + swiglu(x, ffn_w_gate[i], ffn_w_up[i], ffn_w_down[i]), ln2_g[i], ln2_b[i])
    return x
```

```python
import math
from contextlib import ExitStack

import numpy as np
import concourse.bass as bass
import concourse.mybir as mybir
from concourse.bass_utils import run_bass_kernel_spmd

F32 = mybir.dt.float32
BF16 = mybir.dt.bfloat16
AF = mybir.ActivationFunctionType
ALU = mybir.AluOpType
AX = mybir.AxisListType

D_MODEL = 2048
BATCH = 4
SEQ = 2048
DEPTH = 4
N_HEADS = 16
HEAD_DIM = 128
D_FF = 5632
ALPHA = (2 * DEPTH) ** 0.25
LN_EPS = 1e-5
RMS_EPS = 1e-6
NEG = -30000.0

SAME_ENGINE_SYNC = True
N_DMA_SEMS = 12


class K:
    def __init__(self, nc, st):
        self.nc = nc
        self.st = st
        self.E = {"pe": nc.tensor, "dve": nc.vector, "act": nc.scalar,
                  "pool": nc.gpsimd, "sp": nc.sync}
        self.psem = {e: st.enter_context(nc.semaphore("p_" + e)) for e in ("pe", "dve", "act", "pool")}
        self.cnt = {e: 0 for e in self.psem}
        self.pending = {e: False for e in self.psem}
        self.known = {e: {} for e in self.E}
        self.dsem = [st.enter_context(nc.semaphore("d%d" % i)) for i in range(N_DMA_SEMS)]
        self.dcnt = [0] * N_DMA_SEMS
        self.drr = 0
        self.lastw = {}
        self.readers = {}
        self.semobj = {}
        self.nbuf = 0

    def sb(self, shape, dt, name=None):
        self.nbuf += 1
        return self.st.enter_context(self.nc.sbuf_tensor(name or ("sb%d" % self.nbuf), list(shape), dt))

    def ps(self, shape, dt, name=None):
        self.nbuf += 1
        return self.st.enter_context(self.nc.psum_tensor(name or ("ps%d" % self.nbuf), list(shape), dt))

    def _wait(self, eng, ev):
        semid, val, src = ev
        if src == eng and (eng == "pe" or not SAME_ENGINE_SYNC):
            return
        kn = self.known[eng]
        if kn.get(semid, 0) >= val:
            return
        self.E[eng].wait_ge(self.semobj[semid], val)
        kn[semid] = val

    def _deps(self, eng, reads, writes):
        evs = {}
        def add(ev):
            if ev is None:
                return
            if evs.get(ev[0], (0, 0, 0))[1] < ev[1]:
                evs[ev[0]] = ev
        for k in reads:
            add(self.lastw.get(k))
        for k in writes:
            add(self.lastw.get(k))
            for ev in self.readers.get(k, {}).values():
                add(ev)
        for ev in evs.values():
            self._wait(eng, ev)

    def _record(self, ev, reads, writes):
        for k in writes:
            self.lastw[k] = ev
            self.readers[k] = {}
        for k in reads:
            r = self.readers.setdefault(k, {})
            if r.get(ev[0], (0, 0, 0))[1] < ev[1]:
                r[ev[0]] = ev

    def op(self, eng, fn, reads=(), writes=(), inc=True):
        self._deps(eng, reads, writes)
        ins = fn(self.E[eng])
        sem = self.psem[eng]
        sid = "p_" + eng
        self.semobj[sid] = sem
        if inc:
            self.cnt[eng] += 1
            ins.then_inc(sem, 1)
            ev = (sid, self.cnt[eng], eng)
            self.pending[eng] = False
        else:
            ev = (sid, self.cnt[eng] + 1, eng)
            self.pending[eng] = True
        self._record(ev, reads, writes)
        return ins

    def dma(self, q, out, in_, reads=(), writes=(), **kw):
        self._deps(q, reads, writes)
        ins = self.E[q].dma_start(out=out, in_=in_, **kw)
        j = self.drr
        self.drr = (self.drr + 1) % N_DMA_SEMS
        self.dcnt[j] += 16
        ins.then_inc(self.dsem[j], 16)
        sid = "d%d" % j
        self.semobj[sid] = self.dsem[j]
        ev = (sid, self.dcnt[j], None)
        self._record(ev, reads, writes)
        return ev

    def finish(self):
        for j in range(N_DMA_SEMS):
            if self.dcnt[j]:
                self._wait("sp", ("d%d" % j, self.dcnt[j], None))


class WStream:
    def __init__(self, k, slot_elems, nslots, name="wring"):
        self.k = k
        self.nslots = nslots
        self.slot_elems = slot_elems
        self.buf = k.sb([128, nslots * slot_elems], BF16, name)
        self.plan = []
        self.issued = 0
        self.released = 0

    def add(self, src, c, n):
        assert c * n <= self.slot_elems
        self.plan.append((src, c, n))
        return len(self.plan) - 1

    def _issue(self):
        i = self.issued
        src, c, n = self.plan[i]
        s = i % self.nslots
        dst = self.buf[:, s * self.slot_elems: s * self.slot_elems + c * n].rearrange("p (c n) -> p c n", c=c)
        self.k.dma("pool", dst, src, reads=(), writes=[("wring", id(self), s)])
        self.issued += 1

    def _topup(self):
        while self.issued < len(self.plan) and self.issued < self.released + self.nslots:
            self._issue()

    def get(self, ticket):
        self._topup()
        assert ticket < self.issued, (ticket, self.issued, self.released)
        src, c, n = self.plan[ticket]
        s = ticket % self.nslots
        ap = self.buf[:, s * self.slot_elems: s * self.slot_elems + c * n].rearrange("p (c n) -> p c n", c=c)
        return ap, ("wring", id(self), s)

    def done(self, n=1):
        self.released += n
        self._topup()


def layer_norm_tile(k, xt, xkey, gb, gbkey, out_ap, outkey, stats, mv, rstd, tmpkeys):
    skey, mkey, rkey = tmpkeys
    for c in range(4):
        k.op("dve", lambda e, c=c: e.bn_stats(out=stats[:, c, :], in_=xt[:, c * 512:(c + 1) * 512]),
             reads=[xkey], writes=[skey])
    k.op("dve", lambda e: e.bn_aggr(out=mv[:, :], in_=stats[:, :, :]), reads=[skey], writes=[mkey])
    k.op("dve", lambda e: e.tensor_scalar_add(out=rstd[:, :], in0=mv[:, 1:2], scalar1=LN_EPS),
         reads=[mkey], writes=[rkey])
    k.op("act", lambda e: e.sqrt(out=rstd[:, :], in_=rstd[:, :]), reads=[rkey], writes=[rkey])
    k.op("dve", lambda e: e.reciprocal(out=rstd[:, :], in_=rstd[:, :]), reads=[rkey], writes=[rkey])
    k.op("dve", lambda e: e.tensor_scalar(out=out_ap, in0=xt, scalar1=mv[:, 0:1], scalar2=rstd[:, 0:1],
                                          op0=ALU.subtract, op1=ALU.mult), reads=[xkey, mkey, rkey], writes=[outkey])
    k.op("pool", lambda e: e.tensor_tensor(out=out_ap, in0=out_ap, in1=gb[:, 0, :], op=ALU.mult),
         reads=[outkey, gbkey], writes=[outkey])
    k.op("pool", lambda e: e.tensor_tensor(out=out_ap, in0=out_ap, in1=gb[:, 1, :], op=ALU.add),
         reads=[outkey, gbkey], writes=[outkey])


def build_B():
    nc = bass.Bass("TRN2", target_bir_lowering=False)
    NT = 1024
    oT = nc.dram_tensor("oT", [2048, NT], F32, kind="ExternalInput").ap()
    x = nc.dram_tensor("x", [NT, 2048], F32, kind="ExternalInput").ap()
    w_o = nc.dram_tensor("w_o", [2048, 2048], F32, kind="ExternalInput").ap()
    wg = nc.dram_tensor("wg", [2048, D_FF], F32, kind="ExternalInput").ap()
    wu = nc.dram_tensor("wu", [2048, D_FF], F32, kind="ExternalInput").ap()
    wd = nc.dram_tensor("wd", [D_FF, 2048], F32, kind="ExternalInput").ap()
    lnp = nc.dram_tensor("lnp", [4, 2048], F32, kind="ExternalInput").ap()
    ident = nc.dram_tensor("ident", [128, 128], F32, kind="ExternalInput").ap()
    y = nc.dram_tensor("y", [NT, 2048], F32, kind="ExternalOutput").ap()

    with ExitStack() as st:
        k = K(nc, st)
        TP = 512
        NTILE = TP // 128
        xs = k.sb([128, NTILE, 2048], F32, "xs")
        x1T = k.sb([128, 16, TP], BF16, "x1T")
        hT = k.sb([128, 44, TP], BF16, "hT")
        gb = k.sb([128, 2, 2048], F32, "gb")
        oTs = k.sb([128, 16, TP], BF16, "oTs")
        sg = [k.sb([128, 512], F32, "sg%d" % i) for i in range(2)]
        idt = k.sb([128, 128], F32, "idt")
        stats = k.sb([128, 4, 6], F32, "stats")
        mv = k.sb([128, 2], F32, "mv")
        rstd = k.sb([128, 1], F32, "rstd")
        ws = WStream(k, 5632, 4)
        psb = [k.ps([128, 512], F32, "psb%d" % i) for i in range(8)]
        pctr = [0]

        def next_ps():
            i = pctr[0] % 8
            pctr[0] += 1
            return psb[i], ("ps", i)

        k.dma("sp", idt[:, :], ident[:, :], writes=["idt"])

        w_o_v = w_o.rearrange("(c p) n -> p c n", p=128)
        wg_v = wg.rearrange("(c p) n -> p c n", p=128)
        wu_v = wu.rearrange("(c p) n -> p c n", p=128)
        wd_v = wd.rearrange("(c p) n -> p c n", p=128)
        oT_v = oT.rearrange("(c p) n -> p c n", p=128)
        tix = {}
        for ps_ in range(NT // TP):
            for cb in range(8):
                tix[("wo", ps_, cb)] = ws.add(w_o_v[:, :, cb * 256:(cb + 1) * 256], 16, 256)
            for fb in range(22):
                tix[("wg", ps_, fb)] = ws.add(wg_v[:, :, fb * 256:(fb + 1) * 256], 16, 256)
                tix[("wu", ps_, fb)] = ws.add(wu_v[:, :, fb * 256:(fb + 1) * 256], 16, 256)
            for cb in range(4):
                for kq in range(4):
                    tix[("wd", ps_, (cb, kq))] = ws.add(
                        wd_v[:, kq * 11:(kq + 1) * 11, cb * 512:(cb + 1) * 512], 11, 512)

        for ps_ in range(NT // TP):
            t0 = ps_ * TP
            for t in range(NTILE):
                k.dma("sp", xs[:, t, :], x[t0 + t * 128: t0 + (t + 1) * 128, :], writes=[("xs", t)])
            k.dma("act", gb[:, 0, :], lnp[0:1, :].partition_broadcast(128), writes=["gb"])
            k.dma("act", gb[:, 1, :], lnp[1:2, :].partition_broadcast(128), writes=["gb"])
            k.dma("pool", oTs[:, :, :], oT_v[:, :, t0:t0 + TP], writes=["oTs"])
            for cb in range(8):
                W, wkey = ws.get(tix[("wo", ps_, cb)])
                for t in range(NTILE):
                    pb, pkey = next_ps()
                    for c in range(16):
                        k.op("pe", lambda e, c=c, t=t, pb=pb, W=W: e.matmul(
                            pb[:, 0:256], lhsT=oTs[:, c, t * 128:(t + 1) * 128], rhs=W[:, c, :],
                            start=(c == 0), stop=(c == 15)),
                            reads=[wkey, "oTs"], writes=[pkey], inc=(c == 15))
                    sl = xs[:, t, cb * 256:(cb + 1) * 256]
                    k.op("dve", lambda e, sl=sl, pb=pb: e.scalar_tensor_tensor(
                        out=sl, in0=sl, scalar=ALPHA, in1=pb[:, 0:256], op0=ALU.mult, op1=ALU.add),
                        reads=[pkey, ("xs", t)], writes=[("xs", t)])
                ws.done()
            for t in range(NTILE):
                layer_norm_tile(k, xs[:, t, :], ("xs", t), gb, "gb", xs[:, t, :], ("xs", t),
                                stats, mv, rstd, ("stats", "mv", "rstd"))
                for g in range(4):
                    pb, pkey = next_ps()
                    for j in range(4):
                        c = 4 * g + j
                        k.op("pe", lambda e, c=c, j=j, t=t, pb=pb: e.transpose(
                            pb[:, j * 128:(j + 1) * 128], xs[:, t, c * 128:(c + 1) * 128], idt[:, :]),
                            reads=[("xs", t), "idt"], writes=[pkey])
                    k.op("act", lambda e, g=g, t=t, pb=pb: e.copy(
                        out=x1T[:, 4 * g:4 * g + 4, t * 128:(t + 1) * 128],
                        in_=pb[:, :].rearrange("p (j n) -> p j n", j=4)),
                        reads=[pkey], writes=[("x1T", t)])
            x1T_keys = [("x1T", t) for t in range(NTILE)]
            k.dma("act", gb[:, 0, :], lnp[2:3, :].partition_broadcast(128), reads=(), writes=["gb"])
            k.dma("act", gb[:, 1, :], lnp[3:4, :].partition_broadcast(128), reads=(), writes=["gb"])
            for fb in range(22):
                Wg_, gkey = ws.get(tix[("wg", ps_, fb)])
                Wu_, ukey = ws.get(tix[("wu", ps_, fb)])
                for j in range(2):
                    f = fb * 2 + j
                    pg, pgkey = next_ps()
                    pu, pukey = next_ps()
                    for (W, wkey, pb, pkey) in ((Wg_, gkey, pg, pgkey), (Wu_, ukey, pu, pukey)):
                        for c in range(16):
                            k.op("pe", lambda e, c=c, j=j, pb=pb, W=W: e.matmul(
                                pb[:, :], lhsT=W[:, c, j * 128:(j + 1) * 128], rhs=x1T[:, c, :],
                                start=(c == 0), stop=(c == 15)),
                                reads=[wkey] + x1T_keys, writes=[pkey], inc=(c == 15))
                    s = sg[f % 2]
                    skey = ("sg", f % 2)
                    k.op("act", lambda e, s=s, pg=pg: e.activation(out=s[:, :], in_=pg[:, :], func=AF.Silu),
                         reads=[pgkey], writes=[skey])
                    k.op("dve", lambda e, s=s, pu=pu, f=f: e.tensor_tensor(
                        out=hT[:, f, :], in0=s[:, :], in1=pu[:, :], op=ALU.mult),
                        reads=[skey, pukey], writes=[("hT", f)])
                ws.done(2)
            for cb in range(4):
                banks = [next_ps() for _ in range(NTILE)]
                for kq in range(4):
                    W, wkey = ws.get(tix[("wd", ps_, (cb, kq))])
                    for t in range(NTILE):
                        pb, pkey = banks[t]
                        for c in range(11):
                            f = kq * 11 + c
                            k.op("pe", lambda e, c=c, f=f, t=t, pb=pb, W=W, kq=kq: e.matmul(
                                pb[:, :], lhsT=hT[:, f, t * 128:(t + 1) * 128], rhs=W[:, c, :],
                                start=(kq == 0 and c == 0), stop=(kq == 3 and c == 10)),
                                reads=[wkey, ("hT", f)], writes=[pkey], inc=(c == 10))
                    ws.done()
                for t in range(NTILE):
                    pb, pkey = banks[t]
                    sl = xs[:, t, cb * 512:(cb + 1) * 512]
                    k.op("dve", lambda e, sl=sl, pb=pb: e.scalar_tensor_tensor(
                        out=sl, in0=sl, scalar=ALPHA, in1=pb[:, :], op0=ALU.mult, op1=ALU.add),
                        reads=[pkey, ("xs", t)], writes=[("xs", t)])
            for t in range(NTILE):
                layer_norm_tile(k, xs[:, t, :], ("xs", t), gb, "gb", xs[:, t, :], ("xs", t),
                                stats, mv, rstd, ("stats", "mv", "rstd"))
                k.dma("sp", y[t0 + t * 128: t0 + (t + 1) * 128, :], xs[:, t, :], reads=[("xs", t)])
        k.finish()
    return nc


class Attn:
    def __init__(self, k, identb, identb_key, alloc_work=True):
        self.k = k
        self.S = k.ps([128, 2048], F32, "S")
        self.PT = [k.ps([128, 1024], BF16, "PT%d" % i) for i in range(2)]
        self.O = [k.ps([128, 512], F32, "O%d" % i) for i in range(2)]
        if alloc_work:
            self.alloc_work()
        self.identb = identb
        self.identb_key = identb_key
        self.uctr = 0
        self.ptctr = 0
        self.octr = 0
        self.bankctr = 0

    def alloc_work(self):
        k = self.k
        self.sl = [k.sb([128, 2048], F32, "sl%d" % i) for i in range(2)]
        self.P = [k.sb([128, 2048], BF16, "P%d" % i) for i in range(2)]
        self.PTs = [k.sb([128, 16, 128], BF16, "PTs%d" % i) for i in range(2)]
        self.sm = [k.sb([128, 8], F32, "sm%d" % i) for i in range(2)]

    def bank(self):
        i = self.bankctr % 4
        self.bankctr += 1
        return self.S[:, i * 512:(i + 1) * 512], ("S", i)

    def obank(self):
        i = self.octr % 2
        self.octr += 1
        return self.O[i], ("O", i)

    def unit(self, qparts, kparts, v_ap, t, lo, scale, rkeys, strip, sp, cfar=0.0,
             extra=(), dyn=None, sink=None, dv=128, do_pv=True):
        k = self.k
        b = self.uctr % 2
        self.uctr += 1
        S, sl, P, PTs, sm = self.S, self.sl[b], self.P[b], self.PTs[b], self.sm[b]
        slk, Pk, PTk, smk = ("sl", b), ("P", b), ("PTs", b), ("sm", b)
        L = (t + 1 - lo) * 128
        nb = (L + 511) // 512
        Sk = [("S", i) for i in range(nb)]
        for cc in range(0, L, 512):
            n = min(512, L - cc)
            for pi, (qp, kp) in enumerate(zip(qparts, kparts)):
                k.op("pe", lambda e, cc=cc, n=n, qp=qp, kp=kp, pi=pi: e.matmul(
                    S[:, cc:cc + n], lhsT=qp, rhs=kp[:, lo * 128 + cc: lo * 128 + cc + n],
                    start=(pi == 0), stop=(pi == len(qparts) - 1)),
                    reads=rkeys, writes=[("S", cc // 512)], inc=(pi == len(qparts) - 1))
        nstrip = min(sp, t - lo) + 1
        wstrip = nstrip * 128
        far_w = L - wstrip
        if far_w > 0:
            k.op("dve", lambda e: e.tensor_scalar(out=sl[:, 0:far_w], in0=S[:, 0:far_w], scalar1=scale,
                                                  scalar2=cfar, op0=ALU.mult, op1=ALU.add),
                 reads=Sk + rkeys, writes=[slk])
        sw = (sp + 1) * 128
        k.op("dve", lambda e: e.scalar_tensor_tensor(out=sl[:, far_w:L], in0=S[:, far_w:L], scalar=scale,
                                                     in1=strip[:, sw - wstrip: sw], op0=ALU.mult, op1=ALU.add),
             reads=Sk + rkeys, writes=[slk])
        for (kt, ap) in extra:
            if kt >= lo:
                col = (kt - lo) * 128
                k.op("pool", lambda e, col=col, ap=ap: e.tensor_tensor(
                    out=sl[:, col:col + 128], in0=sl[:, col:col + 128], in1=ap, op=ALU.add),
                    reads=[slk] + rkeys, writes=[slk])
        if dyn is not None:
            dap, blk, nbk, dkey = dyn
            if nbk > 0:
                k.op("dve", lambda e: e.tensor_tensor(
                    out=sl[:, 0:nbk * blk].rearrange("p (b j) -> p b j", b=nbk),
                    in0=sl[:, 0:nbk * blk].rearrange("p (b j) -> p b j", b=nbk),
                    in1=dap[:, 0:nbk].unsqueeze(2).to_broadcast([128, nbk, blk]), op=ALU.add),
                    reads=[slk, dkey], writes=[slk])
        k.op("dve", lambda e: e.reduce_max(out=sm[:, 0:1], in_=sl[:, 0:L], axis=AX.X), reads=[slk], writes=[smk])
        if sink is not None:
            k.op("dve", lambda e: e.tensor_tensor(out=sm[:, 0:1], in0=sm[:, 0:1], in1=sink, op=ALU.max),
                 reads=[smk] + rkeys, writes=[smk])
        k.op("dve", lambda e: e.tensor_scalar_mul(out=sm[:, 1:2], in0=sm[:, 0:1], scalar1=-1.0),
             reads=[smk], writes=[smk])
        k.op("dve", lambda e: e.memset(sm[:, 2:3], 0.0), reads=[], writes=[smk])
        k.op("act", lambda e: e.activation(out=P[:, 0:L], in_=sl[:, 0:L], func=AF.Exp, bias=sm[:, 1:2],
                                           scale=1.0, accum_out=sm[:, 2:3]),
             reads=[slk, smk], writes=[Pk, smk])
        if sink is not None:
            k.op("act", lambda e: e.activation(out=sm[:, 3:4], in_=sink, func=AF.Exp, bias=sm[:, 1:2], scale=1.0),
                 reads=[smk] + rkeys, writes=[smk])
            k.op("dve", lambda e: e.tensor_tensor(out=sm[:, 2:3], in0=sm[:, 2:3], in1=sm[:, 3:4], op=ALU.add),
                 reads=[smk], writes=[smk])
        k.op("dve", lambda e: e.reciprocal(out=sm[:, 4:5], in_=sm[:, 2:3]), reads=[smk], writes=[smk])
        ntile = L // 128
        for g in range(0, ntile, 8):
            pi_ = self.ptctr % 2
            self.ptctr += 1
            pt, ptk = self.PT[pi_], ("PT", pi_)
            cnt = min(8, ntile - g)
            for j in range(cnt):
                k.op("pe", lambda e, j=j, g=g, pt=pt: e.transpose(
                    pt[:, j * 128:(j + 1) * 128], P[:, (g + j) * 128:(g + j + 1) * 128], self.identb),
                    reads=[Pk, self.identb_key], writes=[ptk], inc=(j == cnt - 1))
            k.op("act", lambda e, g=g, cnt=cnt, pt=pt: e.copy(
                out=PTs[:, g:g + cnt, :], in_=pt[:, 0:cnt * 128].rearrange("p (j n) -> p j n", j=cnt)),
                reads=[ptk], writes=[PTk])
        if not do_pv:
            return None, None, sm, smk, PTs, PTk
        o, ok = self.obank()
        for j in range(ntile):
            k.op("pe", lambda e, j=j: e.matmul(o[:, 0:dv], lhsT=PTs[:, j, :], rhs=v_ap[:, lo + j, :],
                                               start=(j == 0), stop=(j == ntile - 1)),
                 reads=[PTk] + rkeys, writes=[ok], inc=(j == ntile - 1))
        return o, ok, sm, smk, PTs, PTk


def run_proj(k, ws, att, src_sb, src_keys, C, jobs, tcs=(0, 1, 2, 3), tiles=range(16)):
    for jb in jobs:
        jb["ticket"] = ws.add(jb["src"], C, jb["ncols"])
    ev = [0]

    def evac(out_ap, key, pb, pkey, scale=None):
        e_ = "act" if ev[0] % 2 == 0 else "dve"
        ev[0] += 1
        if e_ == "act":
            k.op("act", lambda e: e.copy(out=out_ap, in_=pb), reads=[pkey], writes=[key])
        else:
            k.op("dve", lambda e: e.tensor_copy(out=out_ap, in_=pb), reads=[pkey], writes=[key])

    for jb in jobs:
        W, wkey = ws.get(jb["ticket"])
        ncols = jb["ncols"]
        if jb["mode"] == "fm":
            for j in range((ncols + 127) // 128):
                M = min(128, ncols - j * 128)
                for tc in tcs:
                    pb, pkey = att.bank()
                    for c in range(C):
                        k.op("pe", lambda e, c=c, j=j, tc=tc, pb=pb, W=W, M=M: e.matmul(
                            pb[0:M, :], lhsT=W[:, c, j * 128: j * 128 + M], rhs=src_sb[:, c, tc * 512:(tc + 1) * 512],
                            start=(c == 0), stop=(c == C - 1)),
                            reads=[wkey] + src_keys, writes=[pkey], inc=(c == C - 1))
                    if "post" in jb:
                        jb["post"](pb, pkey, j, tc)
                    else:
                        out_ap, key = jb["dst"](j, tc)
                        evac(out_ap, key, pb[0:M, :], pkey)
        else:
            for tile in tiles:
                pb, pkey = att.bank()
                for c in range(C):
                    k.op("pe", lambda e, c=c, tile=tile, pb=pb, W=W: e.matmul(
                        pb[:, 0:ncols], lhsT=src_sb[:, c, tile * 128:(tile + 1) * 128], rhs=W[:, c, 0:ncols],
                        start=(c == 0), stop=(c == C - 1)),
                        reads=[wkey] + src_keys, writes=[pkey], inc=(c == C - 1))
                if "post" in jb:
                    jb["post"](pb, pkey, tile)
                else:
                    out_ap, key = jb["dst"](tile)
                    evac(out_ap, key, pb[:, 0:ncols], pkey)
        ws.done()


def barrier(k):
    for e in ("pe", "dve", "act", "pool", "sp"):
        for f in ("pe", "dve", "act", "pool"):
            if f != e and k.cnt[f] > 0:
                k._wait(e, ("p_" + f, k.cnt[f], f))
        for j in range(N_DMA_SEMS):
            if k.dcnt[j]:
                k._wait(e, ("d%d" % j, k.dcnt[j], None))


def build_A_swa():
    nc = bass.Bass("TRN2", target_bir_lowering=False)
    xT = nc.dram_tensor("xT", [2048, 2048], F32, kind="ExternalInput").ap()
    wq = nc.dram_tensor("wq", [2048, 1024], F32, kind="ExternalInput").ap()
    wkv = nc.dram_tensor("wkv", [2048, 256], F32, kind="ExternalInput").ap()
    strip_d = nc.dram_tensor("strip", [128, 8, 256], F32, kind="ExternalInput").ap()
    sinks_d = nc.dram_tensor("sinks", [128, 8], F32, kind="ExternalInput").ap()
    identb_d = nc.dram_tensor("identb", [128, 128], F32, kind="ExternalInput").ap()
    o_d = nc.dram_tensor("o", [2048, 1024], F32, kind="ExternalOutput").ap()
    with ExitStack() as st:
        k = K(nc, st)
        identb = k.sb([128, 128], BF16, "identb_sb")
        k.dma("pool", identb[:, :], identb_d[:, :], writes=["identb"])
        att = Attn(k, identb[:, :], "identb")
        qT = k.sb([128, 8, 2048], BF16, "qT")
        kT = k.sb([128, 2048], BF16, "kT")
        v = k.sb([128, 16, 128], BF16, "v")
        strip = k.sb([128, 8, 256], F32, "strip_sb")
        sinks = k.sb([128, 8], F32, "sinks_sb")
        osb = [k.sb([128, 1024], F32, "osb%d" % i) for i in range(2)]
        k.dma("sp", strip[:, :, :], strip_d[:, :, :], writes=["strip"])
        k.dma("sp", sinks[:, :], sinks_d[:, :], writes=["sinks"])
        with ExitStack() as st2:
            k.st = st2
            xs = k.sb([128, 16, 2048], BF16, "xTs")
            ws = WStream(k, 4096, 4)
            xv = xT.rearrange("(c p) n -> p c n", p=128)
            for i in range(4):
                k.dma("pool", xs[:, 4 * i:4 * i + 4, :], xv[:, 4 * i:4 * i + 4, :], writes=[("xs", i)])
            xkeys = [("xs", i) for i in range(4)]
            jobs = []
            wqv = wq.rearrange("(c p) n -> p c n", p=128)
            wkvv = wkv.rearrange("(c p) n -> p c n", p=128)
            for hp in range(4):
                jobs.append(dict(src=wqv[:, :, hp * 256:(hp + 1) * 256], ncols=256, mode="fm",
                                 dst=lambda j, tc, hp=hp: (qT[:, 2 * hp + j, tc * 512:(tc + 1) * 512], ("qT", 2 * hp + j))))
            jobs.append(dict(src=wkvv[:, :, 0:128], ncols=128, mode="fm",
                             dst=lambda j, tc: (kT[:, tc * 512:(tc + 1) * 512], "kT")))
            jobs.append(dict(src=wkvv[:, :, 128:256], ncols=128, mode="tm",
                             dst=lambda tile: (v[:, tile, :], "v")))
            run_proj(k, ws, att, xs, xkeys, 16, jobs)
            barrier(k)
        k.st = st
        scale = HEAD_DIM ** -0.5
        for t in range(16):
            ob = osb[t % 2]
            obk = ("osb", t % 2)
            for h in range(8):
                lo = max(0, t - 1)
                o, ok, sm, smk, _, _ = att.unit([qT[:, h, t * 128:(t + 1) * 128]], [kT[:, :]], v, t, lo, scale,
                                                rkeys=[("qT", h), "kT", "v", "strip", "sinks"],
                                                strip=strip[:, h, :], sp=1, sink=sinks[:, h:h + 1])
                k.op("act", lambda e, o=o, sm=sm, h=h, ob=ob: e.activation(
                    out=ob[:, h * 128:(h + 1) * 128], in_=o[:, 0:128], func=AF.Copy, scale=sm[:, 4:5]),
                    reads=[ok, smk], writes=[obk])
            k.dma("sp", o_d[t * 128:(t + 1) * 128, :], ob[:, :], reads=[obk])
        k.finish()
    return nc


REL_BUCKETS = 32
REL_MAX_DIST = 128


def t5_bucket_np(dist):
    n = np.maximum(dist, 0)
    max_exact = REL_BUCKETS // 2
    nf = np.maximum(n, max_exact).astype(np.float32)
    large = max_exact + (np.log(nf / np.float32(max_exact)) / np.float32(math.log(REL_MAX_DIST / max_exact))
                         * np.float32(REL_BUCKETS - max_exact)).astype(np.int32)
    return np.where(n < max_exact, n, np.minimum(large, REL_BUCKETS - 1))


def band_strip(rel_bias, heads, n_tiles, window=None):
    i = np.arange(128)[:, None]
    j = np.arange(n_tiles * 128)[None, :]
    dist = i + (n_tiles - 1) * 128 - j
    mask = dist >= 0
    if window is not None:
        mask = mask & (dist < window)
    bidx = t5_bucket_np(dist)
    out = np.empty((128, len(heads), n_tiles * 128), np.float32)
    for a, h in enumerate(heads):
        out[:, a, :] = np.where(mask, rel_bias[bidx, h], np.float32(NEG))
    return out


def ident_f32():
    return np.eye(128, dtype=np.float32)


def rope_pair(k, att, W, wkey, src_sb, src_keys, C, tcs, cos2, sin2, tmp, out_fn):
    for tc in tcs:
        pa, pak = att.bank()
        pb, pbk = att.bank()
        for (pp, ppk, off) in ((pa, pak, 0), (pb, pbk, 128)):
            for c in range(C):
                k.op("pe", lambda e, c=c, pp=pp, off=off, tc=tc: e.matmul(
                    pp[:, :], lhsT=W[:, c, off:off + 128], rhs=src_sb[:, c, tc * 512:(tc + 1) * 512],
                    start=(c == 0), stop=(c == C - 1)),
                    reads=[wkey] + src_keys, writes=[ppk], inc=(c == C - 1))
        ta, tb = tmp
        k.op("dve", lambda e, pa=pa, tc=tc: e.tensor_tensor(out=ta[:, :], in0=pa[:, :], in1=cos2[:, tc * 512:(tc + 1) * 512],
                                                            op=ALU.mult), reads=[pak, "cs"], writes=["ropeA"])
        k.op("dve", lambda e, pb=pb, tc=tc: e.tensor_tensor(out=tb[:, :], in0=pb[:, :], in1=sin2[:, tc * 512:(tc + 1) * 512],
                                                            op=ALU.mult), reads=[pbk, "cs"], writes=["ropeB"])
        out_ap, okey = out_fn(tc)
        k.op("pool", lambda e, out_ap=out_ap: e.tensor_tensor(out=out_ap, in0=ta[:, :], in1=tb[:, :], op=ALU.add),
             reads=["ropeA", "ropeB"], writes=[okey])


def build_A_mla():
    nc = bass.Bass("TRN2", target_bir_lowering=False)
    xT = nc.dram_tensor("xT", [2048, 2048], F32, kind="ExternalInput").ap()
    wdown = nc.dram_tensor("wdown", [2048, 1280], F32, kind="ExternalInput").ap()
    gq_d = nc.dram_tensor("gq", [128, 8], F32, kind="ExternalInput").ap()
    wuq_n = nc.dram_tensor("wuq_n", [512, 1024], F32, kind="ExternalInput").ap()
    wuq_r = nc.dram_tensor("wuq_r", [512, 1024], F32, kind="ExternalInput").ap()
    wukv_k = nc.dram_tensor("wukv_k", [512, 1024], F32, kind="ExternalInput").ap()
    wukv_v = nc.dram_tensor("wukv_v", [512, 1024], F32, kind="ExternalInput").ap()
    cs_d = nc.dram_tensor("cs", [128, 2, 2048], F32, kind="ExternalInput").ap()
    strip_d = nc.dram_tensor("strip", [128, 128], F32, kind="ExternalInput").ap()
    identb_d = nc.dram_tensor("identb", [128, 128], F32, kind="ExternalInput").ap()
    o_d = nc.dram_tensor("o", [2048, 1024], F32, kind="ExternalOutput").ap()
    with ExitStack() as st:
        k = K(nc, st)
        identb = k.sb([128, 128], BF16, "identb_sb")
        k.dma("pool", identb[:, :], identb_d[:, :], writes=["identb"])
        att = Attn(k, identb[:, :], "identb")
        cqn = k.sb([128, 4, 2048], BF16, "cqn")
        ckvn = k.sb([128, 4, 2048], BF16, "ckvn")
        krT2 = k.sb([128, 2048], BF16, "krT2")
        cs = k.sb([128, 2, 2048], F32, "cs_sb")
        strip = k.sb([128, 128], F32, "strip_sb")
        gq = k.sb([128, 8], F32, "gq_sb")
        ones = k.sb([128, 128], BF16, "ones")
        epsT = k.sb([128, 1], F32, "epsT")
        ropetmp = (k.sb([128, 512], F32, "ropeA"), k.sb([128, 512], F32, "ropeB"))
        ws = WStream(k, 4096, 3)
        k.dma("sp", cs[:, :, :], cs_d[:, :, :], writes=["cs"])
        k.dma("sp", strip[:, :], strip_d[:, :], writes=["strip"])
        k.dma("sp", gq[:, :], gq_d[:, :], writes=["gq"])
        k.op("pool", lambda e: e.memset(ones[:, :], 1.0), writes=["ones"])
        k.op("pool", lambda e: e.memset(epsT[:, :], RMS_EPS), writes=["eps"])
        cos2, sin2 = cs[:, 0, :], cs[:, 1, :]
        with ExitStack() as st2:
            k.st = st2
            xs = k.sb([128, 16, 2048], BF16, "xTs")
            c32 = k.sb([128, 8, 512], F32, "c32")
            sq = k.sb([128, 8, 512], BF16, "sq")
            rr = k.sb([128, 512], F32, "rr")
            xv = xT.rearrange("(c p) n -> p c n", p=128)
            for i in range(4):
                k.dma("pool", xs[:, 4 * i:4 * i + 4, :], xv[:, 4 * i:4 * i + 4, :], writes=[("xs", i)])
            xkeys = [("xs", i) for i in range(4)]
            wdv = wdown.rearrange("(c p) n -> p c n", p=128)
            for tc in range(4):
                jobs = []
                for s_ in range(4):
                    jobs.append(dict(src=wdv[:, :, s_ * 256:(s_ + 1) * 256], ncols=256, mode="fm",
                                     dst=lambda j, tc_, s_=s_: (c32[:, 2 * s_ + j, :], ("c32", 2 * s_ + j))))
                run_proj(k, ws, att, xs, xkeys, 16, jobs, tcs=(tc,))
                tk = ws.add(wdv[:, :, 1024:1280], 16, 256)
                W, wkey = ws.get(tk)
                rope_pair(k, att, W, wkey, xs, xkeys, 16, (tc,), cos2, sin2, ropetmp,
                          lambda tc_: (krT2[:, tc_ * 512:(tc_ + 1) * 512], "krT2"))
                ws.done()
                for j in range(8):
                    k.op("act", lambda e, j=j: e.activation(out=sq[:, j, :], in_=c32[:, j, :], func=AF.Square),
                         reads=[("c32", j)], writes=[("sq", j)])
                for part, dstt, dkey in ((0, cqn, "cqn"), (1, ckvn, "ckvn")):
                    pb, pkey = att.bank()
                    for j in range(4):
                        k.op("pe", lambda e, j=j, pb=pb, part=part: e.matmul(
                            pb[:, :], lhsT=ones[:, :], rhs=sq[:, part * 4 + j, :], start=(j == 0), stop=(j == 3)),
                            reads=["ones", ("sq", part * 4 + j)], writes=[pkey], inc=(j == 3))
                    k.op("act", lambda e, pb=pb: e.activation(out=rr[:, :], in_=pb[:, :], func=AF.Sqrt,
                                                              bias=epsT[:, 0:1], scale=1.0 / 512),
                         reads=[pkey, "eps"], writes=["rr"])
                    k.op("dve", lambda e: e.reciprocal(out=rr[:, :], in_=rr[:, :]), reads=["rr"], writes=["rr"])
                    for j in range(4):
                        k.op("dve", lambda e, j=j, part=part, dstt=dstt, tc=tc: e.scalar_tensor_tensor(
                            out=dstt[:, j, tc * 512:(tc + 1) * 512], in0=c32[:, part * 4 + j, :],
                            scalar=gq[:, part * 4 + j: part * 4 + j + 1], in1=rr[:, :], op0=ALU.mult, op1=ALU.mult),
                            reads=[("c32", part * 4 + j), "gq", "rr"], writes=[dkey])
            barrier(k)
        k.st = st
        qn = k.sb([128, 2, 2048], BF16, "qn")
        qr = k.sb([128, 2048], BF16, "qr")
        kn = k.sb([128, 2, 2048], BF16, "kn")
        v = k.sb([128, 16, 256], BF16, "v")
        osb = [k.sb([128, 256], F32, "osb%d" % i) for i in range(2)]
        scale = (128 + 64) ** -0.5
        wqn = wuq_n.rearrange("(c p) n -> p c n", p=128)
        wqr = wuq_r.rearrange("(c p) n -> p c n", p=128)
        wkk = wukv_k.rearrange("(c p) n -> p c n", p=128)
        wkv_ = wukv_v.rearrange("(c p) n -> p c n", p=128)
        for pr in range(4):
            cols = slice(pr * 256, (pr + 1) * 256)
            jobs = [dict(src=wqn[:, :, cols], ncols=256, mode="fm",
                         dst=lambda j, tc: (qn[:, j, tc * 512:(tc + 1) * 512], ("qn", j)))]
            run_proj(k, ws, att, cqn, ["cqn"], 4, jobs)
            tk = ws.add(wqr[:, :, cols], 4, 256)
            W, wkey = ws.get(tk)
            rope_pair(k, att, W, wkey, cqn, ["cqn"], 4, (0, 1, 2, 3), cos2, sin2, ropetmp,
                      lambda tc_: (qr[:, tc_ * 512:(tc_ + 1) * 512], "qr"))
            ws.done()
            jobs = [dict(src=wkk[:, :, cols], ncols=256, mode="fm",
                         dst=lambda j, tc: (kn[:, j, tc * 512:(tc + 1) * 512], ("kn", j))),
                    dict(src=wkv_[:, :, cols], ncols=256, mode="tm",
                         dst=lambda tile: (v[:, tile, :], "v"))]
            run_proj(k, ws, att, ckvn, ["ckvn"], 4, jobs)
            for t in range(16):
                ob = osb[t % 2]
                obk = ("osb", t % 2)
                for hl in range(2):
                    base = 64 * hl
                    o, ok, sm, smk, _, _ = att.unit(
                        [qn[:, hl, t * 128:(t + 1) * 128], qr[base:base + 64, t * 128:(t + 1) * 128]],
                        [kn[:, hl, :], krT2[base:base + 64, :]], v[:, :, hl * 128:(hl + 1) * 128], t, 0, scale,
                        rkeys=[("qn", hl), "qr", ("kn", hl), "krT2", "v", "strip"], strip=strip[:, :], sp=0)
                    k.op("act", lambda e, o=o, sm=sm, hl=hl, ob=ob: e.activation(
                        out=ob[:, hl * 128:(hl + 1) * 128], in_=o[:, 0:128], func=AF.Copy, scale=sm[:, 4:5]),
                        reads=[ok, smk], writes=[obk])
                k.dma("sp", o_d[t * 128:(t + 1) * 128, pr * 256:(pr + 1) * 256], ob[:, :], reads=[obk])
        k.finish()
    return nc


def mla_host_inputs(x_b, hh, w_down, q_norm, kv_norm, w_uq, w_ukv):
    heads = list(range(hh * 8, hh * 8 + 8))
    kr = w_down[:, 1024:1088]
    krs = np.concatenate([kr[:, 32:64], kr[:, 0:32]], axis=1)
    wdown = np.concatenate([w_down[:, :1024], kr, kr, krs, krs], axis=1)
    gq = np.concatenate([q_norm.reshape(4, 128).T, kv_norm.reshape(4, 128).T], axis=1)
    wq3 = w_uq.reshape(512, 16, 192)
    wuq_n = wq3[:, heads, :128].reshape(512, 1024)
    r = wq3[:, heads, 128:192]
    rs = np.concatenate([r[:, :, 32:64], r[:, :, 0:32]], axis=2)
    pairs = []
    for p in range(4):
        pairs += [r[:, 2 * p], r[:, 2 * p + 1], rs[:, 2 * p], rs[:, 2 * p + 1]]
    wuq_r = np.concatenate(pairs, axis=1)
    wk3 = w_ukv.reshape(512, 16, 256)
    wukv_k = wk3[:, heads, :128].reshape(512, 1024)
    wukv_v = wk3[:, heads, 128:].reshape(512, 1024)
    inv = (10000.0 ** (-np.arange(0, 64, 2, dtype=np.float32) / np.float32(64))).astype(np.float32)
    ang = np.arange(2048, dtype=np.float32)[:, None] * inv[None, :]
    cos, sin = np.cos(ang).T.astype(np.float32), np.sin(ang).T.astype(np.float32)
    cos64 = np.concatenate([cos, cos], 0)
    sin64 = np.concatenate([-sin, sin], 0)
    cs = np.stack([np.concatenate([cos64, cos64], 0), np.concatenate([sin64, sin64], 0)], axis=1)
    i = np.arange(128)
    strip = np.where(i[None, :] <= i[:, None], np.float32(0.0), np.float32(NEG)).astype(np.float32)
    c = np.ascontiguousarray
    return dict(xT=c(x_b.T), wdown=c(wdown), gq=c(gq.astype(np.float32)), wuq_n=c(wuq_n), wuq_r=c(wuq_r),
                wukv_k=c(wukv_k), wukv_v=c(wukv_v), cs=c(cs.astype(np.float32)), strip=c(strip), identb=ident_f32())


def build_A_moba():
    nc = bass.Bass("TRN2", target_bir_lowering=False)
    xT = nc.dram_tensor("xT", [2048, 2048], F32, kind="ExternalInput").ap()
    wq = nc.dram_tensor("wq", [2048, 1024], F32, kind="ExternalInput").ap()
    wk = nc.dram_tensor("wk", [2048, 1024], F32, kind="ExternalInput").ap()
    wv = nc.dram_tensor("wv", [2048, 1024], F32, kind="ExternalInput").ap()
    strip_d = nc.dram_tensor("strip", [128, 8, 256], F32, kind="ExternalInput").ap()
    cfar_d = nc.dram_tensor("cfar", [128, 8], F32, kind="ExternalInput").ap()
    identb_d = nc.dram_tensor("identb", [128, 128], F32, kind="ExternalInput").ap()
    o_d = nc.dram_tensor("o", [2048, 1024], F32, kind="ExternalOutput").ap()
    with ExitStack() as st:
        k = K(nc, st)
        identb = k.sb([128, 128], BF16, "identb_sb")
        k.dma("pool", identb[:, :], identb_d[:, :], writes=["identb"])
        att = Attn(k, identb[:, :], "identb")
        strip = k.sb([128, 8, 256], F32, "strip_sb")
        cfar = k.sb([128, 8], F32, "cfar_sb")
        k.dma("sp", strip[:, :, :], strip_d[:, :, :], writes=["strip"])
        k.dma("sp", cfar[:, :], cfar_d[:, :], writes=["cfar"])
        xs = k.sb([128, 16, 2048], BF16, "xTs")
        ws = WStream(k, 4096, 4)
        xv = xT.rearrange("(c p) n -> p c n", p=128)
        for i in range(4):
            k.dma("pool", xs[:, 4 * i:4 * i + 4, :], xv[:, 4 * i:4 * i + 4, :], writes=[("xs", i)])
        xkeys = [("xs", i) for i in range(4)]
        qT = k.sb([128, 2, 2048], BF16, "qT")
        kT = k.sb([128, 2, 2048], BF16, "kT")
        v = k.sb([128, 16, 256], BF16, "v")
        ksum32 = k.sb([128, 8], F32, "ksum32")
        ksum = k.sb([128, 2, 8], BF16, "ksum")
        gsb = k.sb([128, 8], F32, "gsb")
        top = k.sb([128, 8], F32, "top")
        dm = [k.sb([128, 8], F32, "dm%d" % i) for i in range(2)]
        osb = [k.sb([128, 256], F32, "osb%d" % i) for i in range(2)]
        scale = HEAD_DIM ** -0.5
        wqv = wq.rearrange("(c p) n -> p c n", p=128)
        wkv_ = wk.rearrange("(c p) n -> p c n", p=128)
        wvv = wv.rearrange("(c p) n -> p c n", p=128)
        uc = 0
        for pr in range(4):
            cols = slice(pr * 256, (pr + 1) * 256)
            jobs = [dict(src=wqv[:, :, cols], ncols=256, mode="fm",
                         dst=lambda j, tc: (qT[:, j, tc * 512:(tc + 1) * 512], ("qT", j))),
                    dict(src=wkv_[:, :, cols], ncols=256, mode="fm",
                         dst=lambda j, tc: (kT[:, j, tc * 512:(tc + 1) * 512], ("kT", j))),
                    dict(src=wvv[:, :, cols], ncols=256, mode="tm",
                         dst=lambda tile: (v[:, tile, :], "v"))]
            run_proj(k, ws, att, xs, xkeys, 16, jobs)
            for hl in range(2):
                k.op("dve", lambda e, hl=hl: e.tensor_reduce(
                    out=ksum32[:, :], in_=kT[:, hl, :].rearrange("p (b j) -> p b j", b=8), axis=AX.X, op=ALU.add),
                    reads=[("kT", hl)], writes=["ksum32"])
                k.op("dve", lambda e, hl=hl: e.tensor_copy(out=ksum[:, hl, :], in_=ksum32[:, :]),
                     reads=["ksum32"], writes=[("ksum", hl)])
            for t in range(16):
                ob = osb[t % 2]
                obk = ("osb", t % 2)
                qb = t // 2
                for hl in range(2):
                    h = 2 * pr + hl
                    dyn = None
                    if qb > 3:
                        gp, gpk = att.obank()
                        k.op("pe", lambda e, gp=gp, hl=hl, t=t: e.matmul(
                            gp[:, 0:8], lhsT=qT[:, hl, t * 128:(t + 1) * 128], rhs=ksum[:, hl, :], start=True, stop=True),
                            reads=[("qT", hl), ("ksum", hl)], writes=[gpk])
                        d_ = dm[uc % 2]
                        dk = ("dm", uc % 2)
                        uc += 1
                        k.op("dve", lambda e: e.memset(gsb[:, :], NEG), writes=["gsb"])
                        k.op("dve", lambda e, gp=gp, qb=qb: e.tensor_copy(out=gsb[:, 0:qb], in_=gp[:, 0:qb]),
                             reads=[gpk], writes=["gsb"])
                        k.op("dve", lambda e: e.max(out=top[:, :], in_=gsb[:, :]), reads=["gsb"], writes=["top"])
                        k.op("dve", lambda e, d_=d_: e.tensor_scalar(out=d_[:, :], in0=gsb[:, :], scalar1=top[:, 2:3],
                                                                      scalar2=None, op0=ALU.is_ge),
                             reads=["gsb", "top"], writes=[dk])
                        k.op("dve", lambda e, d_=d_: e.tensor_scalar(out=d_[:, :], in0=d_[:, :], scalar1=-1.0,
                                                                      scalar2=-NEG, op0=ALU.add, op1=ALU.mult),
                             reads=[dk], writes=[dk])
                        dyn = (d_, 256, qb, dk)
                    o, ok, sm, smk, _, _ = att.unit(
                        [qT[:, hl, t * 128:(t + 1) * 128]], [kT[:, hl, :]], v[:, :, hl * 128:(hl + 1) * 128], t, 0, scale,
                        rkeys=[("qT", hl), ("kT", hl), "v", "strip", "cfar"], strip=strip[:, h, :], sp=1,
                        cfar=cfar[:, h:h + 1], dyn=dyn)
                    k.op("act", lambda e, o=o, sm=sm, hl=hl, ob=ob: e.activation(
                        out=ob[:, hl * 128:(hl + 1) * 128], in_=o[:, 0:128], func=AF.Copy, scale=sm[:, 4:5]),
                        reads=[ok, smk], writes=[obk])
                k.dma("sp", o_d[t * 128:(t + 1) * 128, pr * 256:(pr + 1) * 256], ob[:, :], reads=[obk])
        k.finish()
    return nc


def moba_host_inputs(x_b, hh, w_qkv, rel_bias):
    heads = list(range(hh * 8, hh * 8 + 8))
    c = np.ascontiguousarray
    sl = slice(hh * 1024, (hh + 1) * 1024)
    return dict(xT=c(x_b.T), wq=c(w_qkv[:, 0:2048][:, sl]), wk=c(w_qkv[:, 2048:4096][:, sl]),
                wv=c(w_qkv[:, 4096:6144][:, sl]), strip=band_strip(rel_bias, heads, 2),
                cfar=c(np.broadcast_to(rel_bias[31, heads][None, :], (128, 8)).astype(np.float32)),
                identb=ident_f32())


GELU_C = 1.5957691216057308
IMP_BIG = 1.0e4


def build_A_nsa():
    nc = bass.Bass("TRN2", target_bir_lowering=False)
    xT = nc.dram_tensor("xT", [2048, 2048], F32, kind="ExternalInput").ap()
    wq = nc.dram_tensor("wq", [2048, 1024], F32, kind="ExternalInput").ap()
    wkv = nc.dram_tensor("wkv", [2048, 1536], F32, kind="ExternalInput").ap()
    wgate = nc.dram_tensor("wgate", [2048, 24], F32, kind="ExternalInput").ap()
    peT_d = nc.dram_tensor("peT", [128, 2, 32], F32, kind="ExternalInput").ap()
    w1_d = nc.dram_tensor("w1", [2, 4096, 256], F32, kind="ExternalInput").ap()
    w2_d = nc.dram_tensor("w2", [2, 256, 128], F32, kind="ExternalInput").ap()
    strip_d = nc.dram_tensor("strip", [128, 8, 256], F32, kind="ExternalInput").ap()
    cfar_d = nc.dram_tensor("cfar", [128, 8], F32, kind="ExternalInput").ap()
    stripc_d = nc.dram_tensor("stripc", [128, 8, 16], F32, kind="ExternalInput").ap()
    tri_d = nc.dram_tensor("tri", [128, 128], F32, kind="ExternalInput").ap()
    addm_d = nc.dram_tensor("addm", [128, 16, 32], F32, kind="ExternalInput").ap()
    ovl_d = nc.dram_tensor("ovl", [128, 32], F32, kind="ExternalInput").ap()
    rowv_d = nc.dram_tensor("rowv", [128, 1], F32, kind="ExternalInput").ap()
    identb_d = nc.dram_tensor("identb", [128, 128], F32, kind="ExternalInput").ap()
    o_d = nc.dram_tensor("o", [2048, 1024], F32, kind="ExternalOutput").ap()
    with ExitStack() as st:
        k = K(nc, st)
        identb = k.sb([128, 128], BF16, "identb_sb")
        k.dma("pool", identb[:, :], identb_d[:, :], writes=["identb"])
        att = Attn(k, identb[:, :], "identb", alloc_work=False)
        qT = k.sb([128, 8, 2048], BF16, "qT")
        kslcT = k.sb([128, 2, 2048], BF16, "kslcT")
        vslc = k.sb([128, 16, 256], BF16, "vslc")
        kwinT = k.sb([128, 2, 2048], BF16, "kwinT")
        vwin = k.sb([128, 16, 256], BF16, "vwin")
        kcT = k.sb([128, 2, 128], BF16, "kcT")
        vc = k.sb([128, 2, 128], BF16, "vc")
        gates = k.sb([128, 16, 24], F32, "gates")
        strip = k.sb([128, 8, 256], F32, "strip_sb")
        cfar = k.sb([128, 8], F32, "cfar_sb")
        stripc = k.sb([128, 8, 16], F32, "stripc_sb")
        tri = k.sb([128, 128], F32, "tri_sb")
        addm = k.sb([128, 16, 32], F32, "addm_sb")
        ovl = k.sb([128, 32], BF16, "ovl_sb")
        rowv = k.sb([128, 1], F32, "rowv_sb")
        peT = k.sb([128, 2, 32], F32, "peT_sb")
        w2s = k.sb([128, 2, 2, 128], BF16, "w2s")
        for (dst, srcd, key, q_) in ((strip[:, :, :], strip_d[:, :, :], "strip", "sp"), (cfar[:, :], cfar_d[:, :], "cfar", "sp"),
                                     (stripc[:, :, :], stripc_d[:, :, :], "stripc", "sp"), (tri[:, :], tri_d[:, :], "tri", "sp"),
                                     (addm[:, :, :], addm_d[:, :, :], "addm", "sp"), (ovl[:, :], ovl_d[:, :], "ovl", "pool"),
                                     (rowv[:, :], rowv_d[:, :], "rowv", "sp"), (peT[:, :, :], peT_d[:, :, :], "peT", "sp"),
                                     (w2s[:, :, :, :], w2_d.rearrange("s (c p) n -> p s c n", p=128), "w2s", "pool")):
            k.dma(q_, dst, srcd, writes=[key])
        scale = HEAD_DIM ** -0.5
        with ExitStack() as st2:
            k.st = st2
            xs = k.sb([128, 16, 2048], BF16, "xTs")
            ws = WStream(k, 4096, 4)
            t32 = k.sb([128, 2048], F32, "t32")
            tA = k.sb([128, 2048], BF16, "tA")
            tB = k.sb([128, 2048], BF16, "tB")
            hidT = k.sb([128, 2, 128], BF16, "hidT")
            g1 = k.sb([128, 128], F32, "g1")
            g2 = k.sb([128, 128], F32, "g2")
            xv = xT.rearrange("(c p) n -> p c n", p=128)
            for i in range(4):
                k.dma("pool", xs[:, 4 * i:4 * i + 4, :], xv[:, 4 * i:4 * i + 4, :], writes=[("xs", i)])
            xkeys = [("xs", i) for i in range(4)]
            wqv = wq.rearrange("(c p) n -> p c n", p=128)
            wkvv = wkv.rearrange("(c p) n -> p c n", p=128)
            wgv = wgate.rearrange("(c p) n -> p c n", p=128)
            jobs = []
            for hp in range(4):
                jobs.append(dict(src=wqv[:, :, hp * 256:(hp + 1) * 256], ncols=256, mode="fm",
                                 dst=lambda j, tc, hp=hp: (qT[:, 2 * hp + j, tc * 512:(tc + 1) * 512], ("qT", 2 * hp + j))))
            jobs.append(dict(src=wkvv[:, :, 512:768], ncols=256, mode="fm",
                             dst=lambda j, tc: (kslcT[:, j, tc * 512:(tc + 1) * 512], ("kslcT", j))))
            jobs.append(dict(src=wkvv[:, :, 768:1024], ncols=256, mode="tm", dst=lambda tile: (vslc[:, tile, :], "vslc")))
            jobs.append(dict(src=wkvv[:, :, 1024:1280], ncols=256, mode="fm",
                             dst=lambda j, tc: (kwinT[:, j, tc * 512:(tc + 1) * 512], ("kwinT", j))))
            jobs.append(dict(src=wkvv[:, :, 1280:1536], ncols=256, mode="tm", dst=lambda tile: (vwin[:, tile, :], "vwin")))

            def gate_post(pb, pkey, tile):
                k.op("act", lambda e: e.activation(out=gates[:, tile, :], in_=pb[:, 0:24], func=AF.Sigmoid),
                     reads=[pkey], writes=["gates"])
            jobs.append(dict(src=wgv[:, :, 0:24], ncols=24, mode="tm", post=gate_post))
            run_proj(k, ws, att, xs, xkeys, 16, jobs)
            for s_ in range(2):
                for kvl in range(2):
                    col0 = s_ * 256 + kvl * 128
                    jobs = [dict(src=wkvv[:, :, col0:col0 + 128], ncols=128, mode="fm",
                                 dst=lambda j, tc: (t32[:, tc * 512:(tc + 1) * 512], "t32"))]
                    run_proj(k, ws, att, xs, xkeys, 16, jobs)
                    for (tt, tkey, off) in ((tA, "tA", 0), (tB, "tB", 16)):
                        k.op("dve", lambda e, tt=tt, off=off, s_=s_: e.tensor_tensor(
                            out=tt[:, :].rearrange("p (g l) -> p g l", l=16),
                            in0=t32[:, :].rearrange("p (g l) -> p g l", l=16),
                            in1=peT[:, s_, off:off + 16].unsqueeze(1).to_broadcast([128, 128, 16]), op=ALU.add),
                            reads=["t32", "peT"], writes=[tkey])
                    w1v = w1_d[s_].rearrange("(l p) n -> p l n", p=128)
                    tks = [ws.add(w1v[:, 0:16, :], 16, 256), ws.add(w1v[:, 16:32, :], 16, 256)]
                    hp_ = [att.bank(), att.bank()]
                    for half in range(2):
                        W, wkey = ws.get(tks[half])
                        tt, tkey = (tA, "tA") if half == 0 else (tB, "tB")
                        for hc in range(2):
                            pb, pkey = hp_[hc]
                            for l in range(16):
                                lg = half * 16 + l
                                k.op("pe", lambda e, l=l, lg=lg, hc=hc, pb=pb, W=W, tt=tt, half=half: e.matmul(
                                    pb[:, 0:127], lhsT=W[:, l, hc * 128:(hc + 1) * 128], rhs=tt[:, lg: lg + 16 * 126 + 1: 16],
                                    start=(half == 0 and l == 0), stop=(half == 1 and l == 15)),
                                    reads=[wkey, tkey], writes=[pkey], inc=(l == 15))
                        ws.done()
                    for hc in range(2):
                        pb, pkey = hp_[hc]
                        xx = pb[:, 0:127]
                        k.op("act", lambda e, xx=xx: e.activation(out=g1[:, 0:127], in_=xx, func=AF.Square),
                             reads=[pkey], writes=["g1"])
                        k.op("dve", lambda e: e.tensor_scalar(out=g1[:, 0:127], in0=g1[:, 0:127], scalar1=0.044715,
                                                              scalar2=1.0, op0=ALU.mult, op1=ALU.add),
                             reads=["g1"], writes=["g1"])
                        k.op("dve", lambda e, xx=xx: e.tensor_tensor(out=g1[:, 0:127], in0=g1[:, 0:127], in1=xx, op=ALU.mult),
                             reads=["g1", pkey], writes=["g1"])
                        k.op("act", lambda e: e.activation(out=g2[:, 0:127], in_=g1[:, 0:127], func=AF.Sigmoid, scale=GELU_C),
                             reads=["g1"], writes=["g2"])
                        k.op("dve", lambda e, xx=xx, hc=hc: e.tensor_tensor(out=hidT[:, hc, 0:127], in0=g2[:, 0:127], in1=xx,
                                                                            op=ALU.mult),
                             reads=["g2", pkey], writes=[("hidT", hc)])
                    pb, pkey = att.bank()
                    if s_ == 0:
                        for hc in range(2):
                            k.op("pe", lambda e, hc=hc, pb=pb: e.matmul(
                                pb[:, 0:127], lhsT=w2s[:, 0, hc, :], rhs=hidT[:, hc, 0:127], start=(hc == 0), stop=(hc == 1)),
                                reads=["w2s", ("hidT", hc)], writes=[pkey], inc=(hc == 1))
                        k.op("act", lambda e, pb=pb, kvl=kvl: e.copy(out=kcT[:, kvl, 0:127], in_=pb[:, 0:127]),
                             reads=[pkey], writes=["kcT"])
                    else:
                        for hc in range(2):
                            k.op("pe", lambda e, hc=hc, pb=pb: e.matmul(
                                pb[0:127, 0:128], lhsT=hidT[:, hc, 0:127], rhs=w2s[:, 1, hc, :], start=(hc == 0), stop=(hc == 1)),
                                reads=["w2s", ("hidT", hc)], writes=[pkey], inc=(hc == 1))
                        k.op("act", lambda e, pb=pb, kvl=kvl: e.copy(out=vc[0:127, kvl, :], in_=pb[0:127, 0:128]),
                             reads=[pkey], writes=["vc"])
            barrier(k)
        k.st = st
        att.alloc_work()
        slc_ = k.sb([128, 128], F32, "slc_")
        Pf = k.sb([128, 128], F32, "Pf")
        Pn = k.sb([128, 128], BF16, "Pn")
        PcT = [k.sb([128, 128], BF16, "PcT%d" % i) for i in range(2)]
        smc = k.sb([128, 8], F32, "smc")
        ocmp = [k.sb([128, 128], F32, "ocmp%d" % i) for i in range(4)]
        acc = [k.sb([128, 128], F32, "acc%d" % i) for i in range(2)]
        coef = k.sb([128, 4], F32, "coef")
        imp2 = k.sb([128, 32], F32, "imp2")
        imp3 = k.sb([128, 32], F32, "imp3")
        top = k.sb([128, 16], F32, "top")
        dm = [k.sb([128, 32], F32, "dm%d" % i) for i in range(2)]
        osb = [k.sb([128, 512], F32, "osb%d" % i) for i in range(2)]
        uc = 0
        for kvl in range(2):
            for t in range(16):
                ob = osb[uc % 2]
                obk = ("osb", uc % 2)
                d_ = dm[uc % 2]
                dk = ("dm", uc % 2)
                uc += 1
                ncols = min(8 * t + 7, 127)
                c_lo = max(0, 8 * t - 9)
                s_lo = c_lo - (8 * t - 9)
                nfar = c_lo
                impb, impk = att.S[:, 1536:2048], ("S", 3)
                for g in range(4):
                    hl = kvl * 4 + g
                    Sb, Sk = att.S[:, (g % 2) * 512:(g % 2) * 512 + 512], ("S", g % 2)
                    k.op("pe", lambda e, Sb=Sb, hl=hl: e.matmul(
                        Sb[:, 0:ncols], lhsT=qT[:, hl, t * 128:(t + 1) * 128], rhs=kcT[:, kvl, 0:ncols], start=True, stop=True),
                        reads=[("qT", hl), "kcT"], writes=[Sk])
                    if nfar > 0:
                        k.op("dve", lambda e, Sb=Sb, hl=hl: e.tensor_scalar(
                            out=slc_[:, 0:nfar], in0=Sb[:, 0:nfar], scalar1=scale, scalar2=cfar[:, hl:hl + 1],
                            op0=ALU.mult, op1=ALU.add), reads=[Sk, "cfar"], writes=["slc_"])
                    k.op("dve", lambda e, Sb=Sb, hl=hl: e.scalar_tensor_tensor(
                        out=slc_[:, nfar:ncols], in0=Sb[:, nfar:ncols], scalar=scale,
                        in1=stripc[:, hl, s_lo:s_lo + (ncols - nfar)], op0=ALU.mult, op1=ALU.add),
                        reads=[Sk, "stripc"], writes=["slc_"])
                    k.op("dve", lambda e: e.reduce_max(out=smc[:, 0:1], in_=slc_[:, 0:ncols], axis=AX.X),
                         reads=["slc_"], writes=["smc"])
                    k.op("dve", lambda e: e.tensor_scalar_mul(out=smc[:, 1:2], in0=smc[:, 0:1], scalar1=-1.0),
                         reads=["smc"], writes=["smc"])
                    k.op("dve", lambda e: e.memset(smc[:, 2:3], 0.0), writes=["smc"])
                    k.op("act", lambda e: e.activation(out=Pf[:, 0:ncols], in_=slc_[:, 0:ncols], func=AF.Exp,
                                                       bias=smc[:, 1:2], scale=1.0, accum_out=smc[:, 2:3]),
                         reads=["slc_", "smc"], writes=["Pf", "smc"])
                    k.op("dve", lambda e: e.reciprocal(out=smc[:, 4:5], in_=smc[:, 2:3]), reads=["smc"], writes=["smc"])
                    if t == 0:
                        k.op("dve", lambda e: e.tensor_tensor(out=smc[:, 4:5], in0=smc[:, 4:5], in1=rowv[:, 0:1], op=ALU.mult),
                             reads=["smc", "rowv"], writes=["smc"])
                    k.op("dve", lambda e: e.tensor_scalar_mul(out=Pn[:, 0:ncols], in0=Pf[:, 0:ncols], scalar1=smc[:, 4:5]),
                         reads=["Pf", "smc"], writes=["Pn"])
                    pi_ = att.ptctr % 2
                    att.ptctr += 1
                    pt, ptk = att.PT[pi_], ("PT", pi_)
                    k.op("pe", lambda e, pt=pt: e.transpose(pt[0:ncols, 0:128], Pn[:, 0:ncols], identb[:, :]),
                         reads=["Pn", "identb"], writes=[ptk])
                    pc, pck = PcT[g % 2], ("PcT", g % 2)
                    k.op("act", lambda e, pt=pt, pc=pc: e.copy(out=pc[0:ncols, :], in_=pt[0:ncols, 0:128]),
                         reads=[ptk], writes=[pck])
                    o, ok = att.obank()
                    k.op("pe", lambda e, o=o, pc=pc: e.matmul(o[:, 0:128], lhsT=pc[0:ncols, :], rhs=vc[0:ncols, kvl, :],
                                                              start=True, stop=True),
                         reads=[pck, "vc"], writes=[ok])
                    k.op("pe", lambda e, pc=pc, g=g: e.matmul(impb[:, 0:32], lhsT=pc[0:ncols, :], rhs=ovl[0:ncols, :],
                                                              start=(g == 0), stop=(g == 3)),
                         reads=[pck, "ovl"], writes=[impk])
                    k.op("act", lambda e, o=o, g=g, hl=hl: e.activation(
                        out=ocmp[g][:, :], in_=o[:, 0:128], func=AF.Copy, scale=gates[:, t, hl * 3:hl * 3 + 1]),
                        reads=[ok, "gates"], writes=[("ocmp", g)])
                k.op("dve", lambda e: e.tensor_tensor(out=imp2[:, :], in0=impb[:, 0:32], in1=addm[:, t, :], op=ALU.add),
                     reads=[impk, "addm"], writes=["imp2"])
                k.op("dve", lambda e: e.max(out=top[:, 0:8], in_=imp2[:, :]), reads=["imp2"], writes=["top"])
                k.op("dve", lambda e: e.match_replace(out=imp3[:, :], in_to_replace=top[:, 0:8], in_values=imp2[:, :],
                                                      imm_value=-3.0e4), reads=["imp2", "top"], writes=["imp3"])
                k.op("dve", lambda e: e.max(out=top[:, 8:16], in_=imp3[:, :]), reads=["imp3"], writes=["top"])
                k.op("dve", lambda e, d_=d_: e.tensor_scalar(out=d_[:, :], in0=imp2[:, :], scalar1=top[:, 15:16], scalar2=None,
                                                              op0=ALU.is_ge), reads=["imp2", "top"], writes=[dk])
                k.op("dve", lambda e, d_=d_: e.tensor_scalar(out=d_[:, :], in0=d_[:, :], scalar1=-1.0, scalar2=-NEG,
                                                              op0=ALU.add, op1=ALU.mult), reads=[dk], writes=[dk])
                for g in range(4):
                    hl = kvl * 4 + g
                    a_ = acc[g % 2]
                    ak = ("acc", g % 2)
                    qap = qT[:, hl, t * 128:(t + 1) * 128]
                    o, ok, sm, smk, _, _ = att.unit(
                        [qap], [kslcT[:, kvl, :]], vslc[:, :, kvl * 128:(kvl + 1) * 128], t, 0, scale,
                        rkeys=[("qT", hl), ("kslcT", kvl), "vslc", "strip", "cfar"], strip=strip[:, hl, :], sp=1,
                        cfar=cfar[:, hl:hl + 1], dyn=(d_, 64, 2 * (t + 1), dk))
                    k.op("dve", lambda e, sm=sm, hl=hl: e.tensor_tensor(out=coef[:, 0:1], in0=sm[:, 4:5],
                                                                        in1=gates[:, t, hl * 3 + 1:hl * 3 + 2], op=ALU.mult),
                         reads=[smk, "gates"], writes=["coef"])
                    k.op("dve", lambda e, o=o, a_=a_, g=g: e.scalar_tensor_tensor(
                        out=a_[:, :], in0=o[:, 0:128], scalar=coef[:, 0:1], in1=ocmp[g][:, :], op0=ALU.mult, op1=ALU.add),
                        reads=[ok, "coef", ("ocmp", g)], writes=[ak])
                    lo = max(0, t - 4)
                    extra = [(t - 4, tri[:, :])] if t >= 4 else []
                    o, ok, sm, smk, _, _ = att.unit(
                        [qap], [kwinT[:, kvl, :]], vwin[:, :, kvl * 128:(kvl + 1) * 128], t, lo, scale,
                        rkeys=[("qT", hl), ("kwinT", kvl), "vwin", "strip", "cfar", "tri"], strip=strip[:, hl, :], sp=1,
                        cfar=cfar[:, hl:hl + 1], extra=extra)
                    k.op("dve", lambda e, sm=sm, hl=hl: e.tensor_tensor(out=coef[:, 1:2], in0=sm[:, 4:5],
                                                                        in1=gates[:, t, hl * 3 + 2:hl * 3 + 3], op=ALU.mult),
                         reads=[smk, "gates"], writes=["coef"])
                    k.op("dve", lambda e, o=o, a_=a_, g=g, ob=ob: e.scalar_tensor_tensor(
                        out=ob[:, g * 128:(g + 1) * 128], in0=o[:, 0:128], scalar=coef[:, 1:2], in1=a_[:, :],
                        op0=ALU.mult, op1=ALU.add), reads=[ok, "coef", ak], writes=[obk])
                k.dma("sp", o_d[t * 128:(t + 1) * 128, kvl * 512:(kvl + 1) * 512], ob[:, :], reads=[obk])
        k.finish()
    return nc


def nsa_host_inputs(x_b, hh, w_in, cmp_pos, cmp_w1, cmp_w2, rel_bias):
    heads = list(range(hh * 8, hh * 8 + 8))
    c = np.ascontiguousarray
    wq = w_in[:, hh * 1024:(hh + 1) * 1024]
    wkv = np.concatenate([w_in[:, 2048 + p * 512 + hh * 256: 2048 + p * 512 + (hh + 1) * 256] for p in range(6)], axis=1)
    wgate = w_in[:, 5120 + hh * 24: 5120 + (hh + 1) * 24]
    peT = np.transpose(cmp_pos, (2, 0, 1))
    i = np.arange(128)[:, None]
    crel = np.arange(-9, 7)[None, :]
    dist = i - 16 * crel - 31
    bidx = t5_bucket_np(dist)
    stripc = np.empty((128, 8, 16), np.float32)
    for a, h in enumerate(heads):
        stripc[:, a, :] = np.where(dist >= 0, rel_bias[bidx, h], np.float32(NEG))
    jj = np.arange(128)[None, :]
    tri = np.where(jj > i, np.float32(0.0), np.float32(NEG)).astype(np.float32)
    addm = np.zeros((128, 16, 32), np.float32)
    for t in range(16):
        qblk = (t * 128 + np.arange(128)) // 64
        j = np.arange(32)[None, :]
        qb = qblk[:, None]
        forced = (j == 0) | (j == qb) | (j == qb - 1)
        addm[:, t, :] = np.where(j > qb, -IMP_BIG, np.where(forced, IMP_BIG, 0.0))
    ci = np.arange(127)[:, None] * 16
    sj = np.arange(32)[None, :] * 64
    ovl = np.zeros((128, 32), np.float32)
    ovl[:127] = ((ci < sj + 64) & (ci + 32 > sj)).astype(np.float32)
    rowv = (np.arange(128) >= 31).astype(np.float32)[:, None]
    return dict(xT=c(x_b.T), wq=c(wq), wkv=c(wkv), wgate=c(wgate), peT=c(peT.astype(np.float32)), w1=c(cmp_w1), w2=c(cmp_w2),
                strip=band_strip(rel_bias, heads, 2),
                cfar=c(np.broadcast_to(rel_bias[31, heads][None, :], (128, 8)).astype(np.float32)),
                stripc=stripc, tri=tri, addm=addm, ovl=ovl, rowv=c(rowv), identb=ident_f32())


_NC_CACHE = {}


def _get_nc(name):
    if name not in _NC_CACHE:
        _NC_CACHE[name] = {"moba": build_A_moba, "swa": build_A_swa, "nsa": build_A_nsa,
                           "mla": build_A_mla, "B": build_B}[name]()
    return _NC_CACHE[name]


def _run(name, in_maps):
    nc = _get_nc(name)
    res = run_bass_kernel_spmd(nc, in_maps, core_ids=list(range(8)))
    return res.results


def swa_host_inputs(x_b, hh, w, sinks, rel_bias):
    heads = list(range(hh * 8, hh * 8 + 8))
    c = np.ascontiguousarray
    return dict(xT=c(x_b.T), wq=c(w[:, hh * 1024:(hh + 1) * 1024]),
                wkv=c(np.concatenate([w[:, 2048 + hh * 128:2048 + (hh + 1) * 128],
                                      w[:, 2304 + hh * 128:2304 + (hh + 1) * 128]], 1)),
                strip=band_strip(rel_bias, heads, 2, window=128),
                sinks=c(np.broadcast_to(sinks[heads][None, :], (128, 8)).astype(np.float32)),
                identb=ident_f32())


def kernel(x, rel_bias, moba_w_qkv, moba_w_o, swa_w_qkv, swa_sinks, swa_w_o,
           nsa_w_in, nsa_cmp_pos, nsa_cmp_w1, nsa_cmp_w2, nsa_w_o,
           mla_w_down, mla_q_norm, mla_kv_norm, mla_w_uq, mla_w_ukv, mla_w_o,
           ln1_g, ln1_b, ffn_w_gate, ffn_w_up, ffn_w_down, ln2_g, ln2_b):
    f = lambda a: np.asarray(a, dtype=np.float32)
    x = f(x)
    rel_bias = f(rel_bias)
    cur = x
    cores = [(b, hh) for b in range(4) for hh in range(2)]
    ident = ident_f32()
    for i in range(DEPTH):
        kind = i % 4
        if kind == 0:
            in_maps = [moba_host_inputs(cur[b], hh, f(moba_w_qkv[0]), rel_bias) for (b, hh) in cores]
            res = _run("moba", in_maps)
            w_o = f(moba_w_o[0])
        elif kind == 1:
            in_maps = [swa_host_inputs(cur[b], hh, f(swa_w_qkv[0]), f(swa_sinks[0]), rel_bias) for (b, hh) in cores]
            res = _run("swa", in_maps)
            w_o = f(swa_w_o[0])
        elif kind == 2:
            in_maps = [nsa_host_inputs(cur[b], hh, f(nsa_w_in[0]), f(nsa_cmp_pos[0]), f(nsa_cmp_w1[0]), f(nsa_cmp_w2[0]),
                                       rel_bias) for (b, hh) in cores]
            res = _run("nsa", in_maps)
            w_o = f(nsa_w_o[0])
        else:
            in_maps = [mla_host_inputs(cur[b], hh, f(mla_w_down[0]), f(mla_q_norm[0]), f(mla_kv_norm[0]),
                                       f(mla_w_uq[0]), f(mla_w_ukv[0])) for (b, hh) in cores]
            res = _run("mla", in_maps)
            w_o = f(mla_w_o[0])
        o_full = [np.concatenate([res[2 * b]["o"], res[2 * b + 1]["o"]], axis=1) for b in range(4)]
        lnp = np.ascontiguousarray(np.stack([f(ln1_g[i]), f(ln1_b[i]), f(ln2_g[i]), f(ln2_b[i])]))
        wg, wu, wd = f(ffn_w_gate[i]), f(ffn_w_up[i]), f(ffn_w_down[i])
        in_maps = []
        for b in range(4):
            for th in range(2):
                tok = slice(th * 1024, (th + 1) * 1024)
                in_maps.append(dict(oT=np.ascontiguousarray(o_full[b][tok].T), x=np.ascontiguousarray(cur[b][tok]),
                                    w_o=w_o, wg=wg, wu=wu, wd=wd, lnp=lnp, ident=ident))
        resB = _run("B", in_maps)
        cur = np.stack([np.concatenate([resB[2 * b]["y"], resB[2 * b + 1]["y"]], axis=0) for b in range(4)])
    return cur.astype(np.float32)
```
